# Optimizing a Trainium2 kernel written in Bass

```python
import math
import jax, jax.numpy as jnp
from jax import lax
import numpy as np

D_MODEL = 1024
BATCH = 4
SEQ = 4096
DEPTH = 4

HEAD_DIM = 64
N_DIFF_HEADS = 6
DIFF_QK_DIM = HEAD_DIM // 2
N_GQA_HEADS = 6
N_GQA_KV_HEADS = 2
GQA_GROUP = N_GQA_HEADS // N_GQA_KV_HEADS
N_MEM_HEADS = 4
N_MEM = 256
GRID_W = 64
Q_BLOCK = 128
ROPE_THETA = 10000.0
NORM_EPS = 1e-6

DIFF_W = N_DIFF_HEADS * HEAD_DIM
GQA_W = N_GQA_HEADS * HEAD_DIM
GQA_KV_W = N_GQA_KV_HEADS * HEAD_DIM
MEM_W = N_MEM_HEADS * HEAD_DIM
D_MIX = DIFF_W + GQA_W + MEM_W
IN_SIZES = [DIFF_W, DIFF_W, DIFF_W, DIFF_W, GQA_W, GQA_KV_W, GQA_KV_W, GQA_W, MEM_W, MEM_W]
D_IN = sum(IN_SIZES)
IN_OFFSETS = [int(v) for v in np.cumsum(IN_SIZES)[:-1]]

kernel_name = "hybrid_diffattn_axialgqa_memxattn_encoder"


def rms_norm(x, g):
    xf = x.astype(jnp.float32)
    y = xf * lax.rsqrt(jnp.mean(xf * xf, axis=-1, keepdims=True) + NORM_EPS)
    return (y * g.astype(jnp.float32)).astype(x.dtype)


def rope_tables(pos, dim):
    inv = ROPE_THETA ** (-jnp.arange(0, dim, 2, dtype=jnp.float32) / dim)
    ang = pos.astype(jnp.float32)[:, None] * inv[None, :]
    return jnp.cos(ang), jnp.sin(ang)


def apply_rope(x, cos, sin):
    xf = x.astype(jnp.float32)
    half = xf.shape[-1] // 2
    x1, x2 = xf[..., :half], xf[..., half:]
    return jnp.concatenate([x1 * cos - x2 * sin, x2 * cos + x1 * sin], axis=-1).astype(x.dtype)


def apply_axial_rope(x, row_cs, col_cs):
    half = x.shape[-1] // 2
    return jnp.concatenate([apply_rope(x[..., :half], *row_cs),
                            apply_rope(x[..., half:], *col_cs)], axis=-1)


def diff_attention(q1, q2, k1, k2, v, lam):
    B, H, S, d = q1.shape
    nb = S // Q_BLOCK
    scale = d ** -0.5

    def blockify(q):
        return jnp.moveaxis(q.reshape(B, H, nb, Q_BLOCK, d), 2, 0)

    def one(qs):
        a, b = qs
        s1 = jnp.einsum('bhqd,bhkd->bhqk', a, k1).astype(jnp.float32) * scale
        s2 = jnp.einsum('bhqd,bhkd->bhqk', b, k2).astype(jnp.float32) * scale
        p = jax.nn.softmax(s1, axis=-1) - lam * jax.nn.softmax(s2, axis=-1)
        return jnp.einsum('bhqk,bhke->bhqe', p.astype(v.dtype), v)

    o = lax.map(one, (blockify(q1), blockify(q2)))
    return jnp.moveaxis(o, 0, 2).reshape(B, H, S, v.shape[-1])


def gqa_attention(q, k, v):
    B, Hk, G, S, d = q.shape
    nb = S // Q_BLOCK
    scale = d ** -0.5
    qb = jnp.moveaxis(q.reshape(B, Hk, G, nb, Q_BLOCK, d), 3, 0)

    def one(qblk):
        s = jnp.einsum('bhgqd,bhkd->bhgqk', qblk, k).astype(jnp.float32) * scale
        p = jax.nn.softmax(s, axis=-1)
        return jnp.einsum('bhgqk,bhkd->bhgqd', p.astype(v.dtype), v)

    o = lax.map(one, qb)
    return jnp.moveaxis(o, 0, 3).reshape(B, Hk, G, S, d)


def setup_inputs(seed: int = 0) -> dict:
    key = jax.random.key(seed)
    ks = jax.random.split(key, 14)
    f32 = jnp.float32
    n = jax.random.normal
    return {
        "x": n(ks[0], (BATCH, SEQ, D_MODEL), f32),
        "mem": n(ks[1], (BATCH, N_MEM, D_MODEL), f32),
        "pre_norm": 1.0 + 0.02 * n(ks[2], (DEPTH, D_MODEL), f32),
        "w_in": n(ks[3], (DEPTH, D_MODEL, D_IN), f32) * D_MODEL ** -0.5,
        "diff_lambda": 0.1 * n(ks[4], (DEPTH, 4, DIFF_QK_DIM), f32),
        "diff_subln": 1.0 + 0.02 * n(ks[5], (DEPTH, HEAD_DIM), f32),
        "gqa_q_norm": 1.0 + 0.02 * n(ks[6], (DEPTH, HEAD_DIM), f32),
        "gqa_k_norm": 1.0 + 0.02 * n(ks[7], (DEPTH, HEAD_DIM), f32),
        "mem_norm": 1.0 + 0.02 * n(ks[8], (DEPTH, D_MODEL), f32),
        "w_mem_kv": n(ks[9], (DEPTH, D_MODEL, 2 * MEM_W), f32) * D_MODEL ** -0.5,
        "w_out": n(ks[10], (DEPTH, D_MIX, D_MODEL), f32) * D_MIX ** -0.5,
        "post_norm": 1.0 + 0.02 * n(ks[11], (DEPTH, D_MODEL), f32),
    }


def reference(x, mem, pre_norm, w_in, diff_lambda, diff_subln, gqa_q_norm, gqa_k_norm,
              mem_norm, w_mem_kv, w_out, post_norm):
    B, S, _ = x.shape
    M = mem.shape[1]
    ROWS = S // GRID_W
    t = jnp.arange(S, dtype=jnp.int32)
    row = jnp.repeat(jnp.arange(ROWS, dtype=jnp.int32), GRID_W)
    col = jnp.tile(jnp.arange(GRID_W, dtype=jnp.int32), ROWS)
    lin_cs = rope_tables(t, DIFF_QK_DIM)
    row_cs = rope_tables(row, HEAD_DIM // 2)
    col_cs = rope_tables(col, HEAD_DIM // 2)

    for l in range(DEPTH):
        lambda_init = 0.8 - 0.6 * math.exp(-0.3 * l)
        h = rms_norm(x, pre_norm[l])
        proj = h @ w_in[l]
        (dq, dk, dv, dgate, gq, gk, gv, ggate, mq, mgate) = jnp.split(proj, IN_OFFSETS, axis=-1)

        dq = apply_rope(dq.reshape(B, S, N_DIFF_HEADS, 2, DIFF_QK_DIM).transpose(0, 2, 3, 1, 4), *lin_cs)
        dk = apply_rope(dk.reshape(B, S, N_DIFF_HEADS, 2, DIFF_QK_DIM).transpose(0, 2, 3, 1, 4), *lin_cs)
        dv = dv.reshape(B, S, N_DIFF_HEADS, HEAD_DIM).transpose(0, 2, 1, 3)
        lp = diff_lambda[l].astype(jnp.float32)
        lam = jnp.exp(jnp.sum(lp[0] * lp[1])) - jnp.exp(jnp.sum(lp[2] * lp[3])) + lambda_init
        oa = diff_attention(dq[:, :, 0], dq[:, :, 1], dk[:, :, 0], dk[:, :, 1], dv, lam)
        oa = rms_norm(oa, diff_subln[l]) * (1.0 - lambda_init)
        oa = oa.transpose(0, 2, 1, 3).reshape(B, S, DIFF_W) * jax.nn.silu(dgate)

        gq = rms_norm(gq.reshape(B, S, N_GQA_HEADS, HEAD_DIM).transpose(0, 2, 1, 3), gqa_q_norm[l])
        gq = apply_axial_rope(gq, row_cs, col_cs).reshape(B, N_GQA_KV_HEADS, GQA_GROUP, S, HEAD_DIM)
        gk = rms_norm(gk.reshape(B, S, N_GQA_KV_HEADS, HEAD_DIM).transpose(0, 2, 1, 3), gqa_k_norm[l])
        gk = apply_axial_rope(gk, row_cs, col_cs)
        gv = gv.reshape(B, S, N_GQA_KV_HEADS, HEAD_DIM).transpose(0, 2, 1, 3)
        ob = gqa_attention(gq, gk, gv).reshape(B, N_GQA_HEADS, S, HEAD_DIM)
        ob = ob.transpose(0, 2, 1, 3).reshape(B, S, GQA_W) * jax.nn.silu(ggate)

        mkv = rms_norm(mem, mem_norm[l]) @ w_mem_kv[l]
        mk, mv = jnp.split(mkv, 2, axis=-1)
        mk = mk.reshape(B, M, N_MEM_HEADS, HEAD_DIM).transpose(0, 2, 1, 3)
        mv = mv.reshape(B, M, N_MEM_HEADS, HEAD_DIM).transpose(0, 2, 1, 3)
        mq = mq.reshape(B, S, N_MEM_HEADS, HEAD_DIM).transpose(0, 2, 1, 3)
        sm = jnp.einsum('bhsd,bhmd->bhsm', mq, mk).astype(jnp.float32) * HEAD_DIM ** -0.5
        pm = jax.nn.softmax(sm, axis=-1)
        om = jnp.einsum('bhsm,bhmd->bhsd', pm.astype(mv.dtype), mv)
        om = om.transpose(0, 2, 1, 3).reshape(B, S, MEM_W) * jax.nn.silu(mgate)

        y = jnp.concatenate([oa, ob, om], axis=-1) @ w_out[l]
        x = x + rms_norm(y, post_norm[l])
    return x
```

```python
import contextlib
import math
import numpy as np
import ml_dtypes
import concourse.bass as bass
import concourse.mybir as mybir
from concourse.bass_utils import run_bass_kernel_spmd

F32 = mybir.dt.float32
BF16 = mybir.dt.bfloat16
ALU = mybir.AluOpType
AF = mybir.ActivationFunctionType
AX = mybir.AxisListType

ENGS = ("pe", "act", "dve", "pool", "sp")
D = 1024
S_OWN = 2048
NT_OWN = 16
NBG_KV = 0
CH = 512
NCH = S_OWN // CH
TPC = CH // 128
FK_KT = 4 * 2048
FK_V = 16 * 8 * 65
FK = FK_KT + FK_V
EPS = 1e-6


class Buf:
    __slots__ = ("name", "w", "r", "excl")

    def __init__(self, name, excl=False):
        self.name = name
        self.w = None
        self.r = {}
        self.excl = excl


class Sched:
    def __init__(self, sems, dma_sems, n_sw):
        self.sem = sems
        self.cnt = {e: 0 for e in ENGS}
        self.prog = {e: [] for e in ENGS}
        self.seen = {e: {} for e in ENGS}
        self.dma_sems = dma_sems
        self.dma_cnt = [0] * len(dma_sems)
        self.n_sw = n_sw
        self.dma_rr = {True: 0, False: 0}

    def _need(self, eng, tickets):
        waits = {}
        for t in tickets:
            if t is None:
                continue
            key, sh, val = t
            if key == eng and eng == "pe":
                continue
            if self.seen[eng].get(key, 0) >= val:
                continue
            if waits.get(key, (None, 0))[1] < val:
                waits[key] = (sh, val)
        for key, (sh, val) in waits.items():
            self.seen[eng][key] = val
            self.prog[eng].append(lambda e, sh=sh, val=val: e.wait_ge(sh, val))

    def _deps(self, eng, rd, wr):
        tickets = []
        for b in rd:
            tickets.append(b.w)
            if b.excl:
                tickets.extend(t for k, t in b.r.items() if k != eng)
        for b in wr:
            tickets.append(b.w)
            tickets.extend(b.r.values())
        return tickets

    def _mark(self, t, rd, wr):
        key = t[0]
        for b in rd:
            old = b.r.get(key)
            if old is None or old[2] < t[2]:
                b.r[key] = t
        for b in wr:
            b.w = t
            b.r = {}

    def op(self, eng, fn, rd=(), wr=(), inc=True):
        self._need(eng, self._deps(eng, rd, wr))
        sh = self.sem[eng]
        if inc:
            self.cnt[eng] += 1
            val = self.cnt[eng]
            self.prog[eng].append(lambda e, fn=fn, sh=sh: fn(e).then_inc(sh, 1))
        else:
            val = self.cnt[eng] + 1
            self.prog[eng].append(lambda e, fn=fn: fn(e))
        t = (eng, sh, val)
        self._mark(t, rd, wr)
        return t

    def dma(self, eng, out, in_, rd=(), wr=(), **kw):
        sw = eng == "pool"
        k = self.dma_rr[sw]
        n = self.n_sw if sw else len(self.dma_sems) - self.n_sw
        self.dma_rr[sw] = (k + 1) % n
        i = k if sw else self.n_sw + k
        sh = self.dma_sems[i]
        key = ("d", i)
        tickets = self._deps(key, rd, wr)
        if self.dma_cnt[i]:
            tickets.append((key, sh, self.dma_cnt[i]))
        self._need(eng, tickets)
        self.dma_cnt[i] += 16
        val = self.dma_cnt[i]
        self.prog[eng].append(
            lambda e, out=out, in_=in_, kw=kw, sh=sh: e.dma_start(out=out, in_=in_, **kw).then_inc(sh, 16))
        t = (key, sh, val)
        self._mark(t, rd, wr)
        return t

    def custom(self, eng, fn, sh, rd=(), wr=()):
        self._need(eng, self._deps(("c", id(sh)), rd, wr))
        self.prog[eng].append(lambda e, fn=fn, sh=sh: fn(e).then_inc(sh, 1))
        t = (("c", id(sh)), sh, 1)
        self._mark(t, rd, wr)
        return t

    def wait_all(self, eng, bufs):
        self._need(eng, [b.w for b in bufs])

    def emit(self, block):
        hooks = {"pe": block.tensor, "act": block.scalar, "dve": block.vector,
                 "pool": block.gpsimd, "sp": block.sync}
        for eng in ENGS:
            prog = self.prog[eng]

            def body(e, prog=prog):
                for f in prog:
                    f(e)
            hooks[eng](body)


class Ring:
    def __init__(self, items):
        self.items = items
        self.i = 0

    def next(self):
        it = self.items[self.i % len(self.items)]
        self.i += 1
        return it


def bc(ap, axis, shape):
    return ap.unsqueeze(axis).broadcast_to(list(shape))


def drain(gen):
    for _ in gen:
        pass


def rr(*gens):
    gens = [g for g in gens if g is not None]
    while gens:
        for g in list(gens):
            try:
                next(g)
            except StopIteration:
                gens.remove(g)
        yield


def chain(*gens):
    for g in gens:
        if g is not None:
            yield from g


def build(nl):
    nc = bass.Bass("TRN2", target_bir_lowering=False)
    dt_ = nc.dram_tensor
    xown = dt_("xown", [S_OWN, D], F32, kind="ExternalInput").ap()
    mem = dt_("mem", [256, D], F32, kind="ExternalInput").ap()
    w_in = dt_("w_in", [nl, D, 3072], F32, kind="ExternalInput").ap()
    w_mkv = dt_("w_mkv", [nl, D, 512], F32, kind="ExternalInput").ap()
    w_out = dt_("w_out", [nl, D, D], F32, kind="ExternalInput").ap()
    pre_n = dt_("pre_n", [nl, D], F32, kind="ExternalInput").ap()
    mem_n = dt_("mem_n", [nl, D], F32, kind="ExternalInput").ap()
    post_n = dt_("post_n", [nl, D], F32, kind="ExternalInput").ap()
    dlam = dt_("dlam", [nl, 128], F32, kind="ExternalInput").ap()
    sgn = dt_("sgn", [nl, 192], F32, kind="ExternalInput").ap()
    lini = dt_("lini", [nl, 2], F32, kind="ExternalInput").ap()
    tabo = dt_("tabo", [NT_OWN, 128, 192], F32, kind="ExternalInput").ap()
    ident_d = dt_("ident", [128, 128], BF16, kind="ExternalInput").ap()
    identf_d = dt_("identf", [128, 128], F32, kind="ExternalInput").ap()
    out = dt_("out", [S_OWN, D], F32, kind="ExternalOutput").ap()
    NPAR = min(2, nl)
    ksnd = [[dt_(f"ksnd{i}_{p}", [128, 4 * 512], BF16).ap() for p in range(4)] for i in range(NPAR)]
    krcv = [[dt_(f"krcv{i}_{p}", [256, 4 * 512], BF16).ap() for p in range(4)] for i in range(NPAR)]
    vsnd = [[dt_(f"vsnd{i}_{p}", [128, 4 * 520], BF16).ap() for p in range(4)] for i in range(NPAR)]
    vrcv = [[dt_(f"vrcv{i}_{p}", [256, 4 * 520], BF16).ap() for p in range(4)] for i in range(NPAR)]

    with contextlib.ExitStack() as st:
        E = st.enter_context

        def sb(name, shape, dt):
            return E(nc.sbuf_tensor(name, shape, dt))

        KT = sb("KT", [128, 4, 4096], BF16)
        Vflat = sb("V", [128, 32 * 520 + 64], BF16)
        V = Vflat[:, 0:32 * 520].rearrange("p (t h d) -> p t h d", t=32, h=8)
        QTs = [sb(f"QT{i}", [128, 11, CH], BF16) for i in range(2)]
        ccs = [sb(f"cc{i}", [128, TPC, D], BF16) for i in range(2)]
        xTcs = [sb(f"xTc{i}", [128, TPC, 8, 128], BF16) for i in range(2)]
        Wr = [sb(f"W{i}", [128, 8, 512], BF16) for i in range(2)]
        Er = [sb(f"E{i}", [128, 1024], BF16) for i in range(3)]
        xsr = [sb(f"xs{i}", [128, D], F32) for i in range(3)]
        hbr = [sb(f"hb{i}", [128, D], BF16) for i in range(2)]
        TT = [sb(f"tt{i}", [128, 1024], F32) for i in range(3)]
        T3 = [sb(f"t3_{i}", [128, 384], F32) for i in range(2)]
        tts = [[TT[i][:, 0:512], TT[i][:, 512:1024], T3[i][:] if i < 2 else None] for i in range(3)]
        TT = list(TT)
        hA = sb("hA", [128, 512], F32)
        stgQ = [[sb(f"stgQ{i}{k}", [128, 512], BF16) for k in range(2)] for i in range(2)]
        stg = [sb(f"stg{i}", [128, 512], BF16) for i in range(2)]
        ktst = [sb(f"ktst{i}", [128, 4, 128], BF16) for i in range(2)]
        vst = [sb(f"vst{i}", [128, 8, 65], BF16) for i in range(2)]
        osb = sb("osb", [128, 1024], F32)
        tabr = [sb(f"tab{i}", [128, 192], F32) for i in range(3)]
        gpre = sb("gpre", [128, D], F32)
        gpost = sb("gpost", [128, D], F32)
        dl = sb("dl", [128, 128], F32)
        sg = sb("sg", [128, 192], F32)
        sg2 = sb("sg2", [128, 64], F32)
        lin_t = sb("lin_t", [128, 2], F32)
        small = [sb(f"sm{i}", [128, 16], F32) for i in range(16)]
        nlam = sb("nlam", [128, 1], F32)
        eps_t = sb("eps_t", [128, 1], F32)
        KmT = sb("KmT", [128, 2, 256], BF16)
        Vmflat = sb("Vm", [128, 2 * 260 + 64], BF16)
        Vm = Vmflat[:, 0:2 * 260].rearrange("p (t h d) -> p t h d", t=2, h=4)
        ysb = sb("ysb", [128, D], F32)
        gmem = ysb
        ident = sb("identsb", [128, 128], BF16)
        identf = sb("identfsb", [128, 128], F32)
        PS = [E(nc.psum_tensor(f"PS{i}", [128, 2, 512], F32)) for i in range(4)]
        sems = {e: E(nc.semaphore(f"s_{e}")) for e in ENGS}
        dsems = [E(nc.semaphore(f"d{i}")) for i in range(32)]
        ccsems = [E(nc.semaphore(f"cc{i}")) for i in range(nl * 8)]
        block = E(nc.Block())
        S = Sched(sems, dsems, 12)

        bKT, bV = Buf("KT"), Buf("V")
        bQTs = [Buf("QT0"), Buf("QT1")]
        bccs = [Buf("cc0"), Buf("cc1")]
        bxTs = [[Buf(f"xT{p}_{i}") for i in range(TPC)] for p in range(2)]
        Wring = Ring([(Wr[i], Buf(f"W{i}")) for i in range(2)])
        Ering = Ring([(Er[i], Buf(f"E{i}")) for i in range(3)])
        xsring = Ring([(xsr[i], Buf(f"xs{i}")) for i in range(3)])
        hbring = Ring([(hbr[i], Buf(f"hb{i}")) for i in range(2)])
        stgring = Ring([(stg[i], Buf(f"stg{i}")) for i in range(2)])
        bstgQ = [Buf("stgQ0"), Buf("stgQ1")]
        bhA = Buf("hA")
        ktring = Ring([(ktst[i], Buf(f"ktst{i}")) for i in range(2)])
        vring = Ring([(vst[i], Buf(f"vst{i}")) for i in range(2)])
        bosb = Buf("osb")
        tabring = Ring([(tabr[i][:], Buf(f"tab{i}")) for i in range(3)])
        smring = Ring([(small[i], Buf(f"sm{i}")) for i in range(16)])
        btts = [[Buf(f"t{k}_{i}") for k in range(3)] for i in range(3)]
        bgpre, bgpost, bdl, bsg, bsg2, blin, bnlam = (Buf(n) for n in
                                                      ["gpre", "gpost", "dl", "sg", "sg2", "lin", "nlam"])
        bconst = Buf("const")
        bKmT, bVm, bysb = Buf("KmT"), Buf("Vm"), Buf("ysb")
        bgmem = bysb
        ysb_main, bysb_main = ysb, bysb
        bank = [[PS[i][:, j, :] for j in range(2)] for i in range(4)]
        bbank = [[Buf(f"ps{i}{j}", excl=True) for j in range(2)] for i in range(4)]
        bxrow = [Buf(f"xrow{i}") for i in range(NT_OWN)]
        bkvsK = [[Buf(f"kvsK{i}_{t}") for t in range(NT_OWN)] for i in range(2)]
        bkvsV = [[Buf(f"kvsV{i}_{t}") for t in range(NT_OWN)] for i in range(2)]
        bkrcv = [[Buf(f"krcv{i}_{p}") for p in range(4)] for i in range(2)]
        bvrcv = [[Buf(f"vrcv{i}_{p}") for p in range(4)] for i in range(2)]

        tts[2][2] = hA[:, 0:384]
        btts[2][2] = bhA
        TT.append(osb)
        tts.append([osb[:, 0:512], osb[:, 512:1024], Er[0][:].bitcast(F32)[:, 0:384]])
        btts.append([bosb, bosb, Ering.items[0][1]])
        xs_kv = Ring([(xsr[0], xsring.items[0][1]), (xsr[1], xsring.items[1][1]), (xsr[2], xsring.items[2][1]),
                      (ysb, bysb)])
        hb_kv = Ring([hbring.items[0], hbring.items[1], (Er[1], Ering.items[1][1]), (Er[2], Ering.items[2][1])])
        stg_kv = Ring([stgring.items[0], stgring.items[1], (QTs[1][:, 0, :], bQTs[1]), (QTs[1][:, 1, :], bQTs[1])])
        tab_kv = Ring([tabring.items[0], tabring.items[1], tabring.items[2],
                       (ccs[1][:, 0, :].bitcast(F32)[:, 0:192], bccs[1])])
        BS_KV4 = [dict(T1=(i, 0), PA=(i, 1), PB=(i, 1), T2=(i, 0)) for i in range(4)]

        BS_SEQ = [dict(T1=(3, 0), PA=(2, 0), PB=(2, 1), T2=(3, 1)), dict(T1=(1, 0), PA=(0, 0), PB=(0, 1), T2=(1, 1))]
        BS_BG = [dict(T1=(3, 1), PA=(3, 1), PB=(3, 1), T2=(3, 1))] * 2

        def bk(ix):
            return bank[ix[0]][ix[1]], bbank[ix[0]][ix[1]]

        def bf16view(bkap):
            return bkap.bitcast(BF16).rearrange("p (a b) -> p a b", a=8)

        S.dma("sp", ident[:], ident_d, wr=[bconst])
        S.dma("sp", identf[:], identf_d, wr=[bconst])
        S.op("pool", lambda e: e.memset(eps_t[:], EPS), wr=[bconst])
        S.op("pool", lambda e: e.memset(Vflat[:], 1.0), wr=[bV])
        S.op("pool", lambda e: e.memset(Vmflat[:], 1.0), wr=[bVm])
        for i in range(2):
            S.op("pool", lambda e, i=i: e.memset(vst[i][:], 1.0), wr=[vring.items[i][1]])
            for k in range(2):
                S.op("pool", lambda e, i=i, k=k: e.memset(stgQ[i][k][:], 0.0), wr=[bstgQ[i]])

        def rstd_of(ss_ap, n, bss, cols):
            sm, bsm = smring.next()
            S.op("act", lambda e: e.activation(out=sm[:, 0:cols], in_=ss_ap, func=AF.Ln, scale=1.0 / n,
                                               bias=eps_t[:, 0:1]), rd=[bss, bconst], wr=[bsm])
            S.op("act", lambda e: e.activation(out=sm[:, 8:8 + cols], in_=sm[:, 0:cols], func=AF.Exp, scale=-0.5),
                 rd=[bsm], wr=[bsm])
            return sm[:, 8:8 + cols], bsm

        def xload(src_ap, rd_src, tab_src=None, xr=None, tr=None):
            xs, bxs = (xr or xsring).next()
            S.dma("sp", xs[:], src_ap, rd=rd_src, wr=[bxs])
            tb = None
            if tab_src is not None:
                tb = (tr or tabring).next()
                S.dma("sp", tb[0], tab_src, wr=[tb[1]])
            return xs, bxs, tb

        def g_front(ld, gain, bgain, dst_ap, bdst, tbank, ts, hbr=None, alt="dve"):
            xs, bxs = ld[0], ld[1]
            pbank, bpbank = bk(tbank)
            sqj, bsq = TT[ts], [btts[ts][0], btts[ts][1]]
            sm, bsm = smring.next()
            if alt == "act":
                S.op("act", lambda e: e.activation(out=sqj[:], in_=xs[:], func=AF.Square, accum_out=sm[:, 0:1]),
                     rd=[bxs], wr=bsq + [bsm])
            else:
                S.op("dve", lambda e: e.tensor_tensor(out=sqj[:], in0=xs[:], in1=xs[:], op=ALU.mult), rd=[bxs], wr=bsq)
                S.op("dve", lambda e: e.reduce_sum(out=sm[:, 0:1], in_=sqj[:], axis=AX.X), rd=bsq, wr=[bsm])
            yield
            yield
            yield
            yield
            rs, brs = rstd_of(sm[:, 0:1], D, bsm, 1)
            yield
            yield
            hb, bhb = (hbr or hbring).next()
            S.op("dve", lambda e: e.scalar_tensor_tensor(out=hb[:], in0=xs[:], scalar=rs, in1=gain[:],
                                                         op0=ALU.mult, op1=ALU.mult),
                 rd=[bxs, brs, bgain], wr=[bhb])
            yield
            yield
            yield
            yield
            pv = bf16view(pbank)
            for kc in range(8):
                S.op("pe", lambda e, kc=kc: e.transpose(out=pv[:, kc, :], in_=hb[:, kc * 128:(kc + 1) * 128],
                                                        identity=ident[:]),
                     rd=[bhb, bconst], wr=[bpbank], inc=(kc == 7))
            yield
            yield
            if alt == "act":
                S.op("act", lambda e: e.copy(out=dst_ap, in_=pv), rd=[bpbank], wr=[bdst])
            else:
                S.op("dve", lambda e: e.tensor_copy(out=dst_ap, in_=pv), rd=[bpbank], wr=[bdst])
            yield
            yield
            yield

        def wload(parts):
            w, bw = Wring.next()
            for src, c0 in parts:
                n = src.shape[1]
                S.dma("pool", w[:, :, c0:c0 + n], src.rearrange("(kc p) n -> p kc n", p=128), wr=[bw])
            return w, bw

        def proj(xT_ap, bx, w, bw, pix):
            pbank, bpbank = bk(pix)
            for kc in range(8):
                S.op("pe", lambda e, kc=kc: e.matmul(out=pbank, lhsT=xT_ap[:, kc, :], rhs=w[:, kc, :],
                                                     start=(kc == 0), stop=(kc == 7)),
                     rd=[bx, bw], wr=[bpbank], inc=(kc == 7))
            return pbank, bpbank

        def rope(ts, src, rd_src, G, inner, C, Sg, btab, dst, bdst):
            (t1, t2, _), (bt1, bt2, _) = tts[ts], btts[ts]
            n = G * inner * 32
            v4 = lambda ap: ap.rearrange("p (g i d) -> p g i d", g=G, i=inner)
            sv, t1v, t2v = v4(src), v4(t1[:, 0:n]), v4(t2[:, 0:n])
            Cv = bc(C.rearrange("p (i d) -> p i d", i=inner), 1, [128, G, inner, 32])
            Sv = Sg.rearrange("p (i d) -> p i d", i=inner)
            S.op("dve", lambda e: e.tensor_tensor(out=t1v, in0=sv, in1=Cv, op=ALU.mult), rd=rd_src + [btab], wr=[bt1])
            S.op("dve", lambda e: e.tensor_tensor(out=t2v[:, :, :, 0:16], in0=sv[:, :, :, 16:32],
                                                  in1=bc(Sv[:, :, 0:16], 1, [128, G, inner, 16]), op=ALU.mult),
                 rd=rd_src + [btab], wr=[bt2])
            S.op("dve", lambda e: e.tensor_tensor(out=t2v[:, :, :, 16:32], in0=sv[:, :, :, 0:16],
                                                  in1=bc(Sv[:, :, 16:32], 1, [128, G, inner, 16]), op=ALU.mult),
                 rd=rd_src + [btab], wr=[bt2])
            if dst is not None:
                S.op("dve", lambda e: e.tensor_tensor(out=dst, in0=t1[:, 0:n], in1=t2[:, 0:n], op=ALU.add),
                     rd=[bt1, bt2], wr=[bdst])

        def g_gqa_norm_rope(ts, src, rd_src, H, gain_ap, tab, btab, final, bdst):
            (t1, t2, t3), (bt1, bt2, bt3) = tts[ts], btts[ts]
            n = H * 64
            S.op("dve", lambda e: e.tensor_copy(out=t3[:, 0:n], in_=src), rd=rd_src, wr=[bt3])
            S.op("dve", lambda e: e.tensor_tensor(out=t1[:, 0:n], in0=t3[:, 0:n], in1=t3[:, 0:n], op=ALU.mult),
                 rd=[bt3], wr=[bt1])
            sm, bsm = smring.next()
            S.op("dve", lambda e: e.reduce_sum(out=sm[:, 0:H], in_=t1[:, 0:n].rearrange("p (h d) -> p h d", h=H),
                                               axis=AX.X), rd=[bt1], wr=[bsm])
            yield
            yield
            yield
            yield
            rs, brs = rstd_of(sm[:, 0:H], 64, bsm, H)
            S.op("dve", lambda e: e.tensor_tensor(out=t3[:, 0:n].rearrange("p (h d) -> p h d", h=H),
                                                  in0=t3[:, 0:n].rearrange("p (h d) -> p h d", h=H),
                                                  in1=bc(gain_ap, 1, [128, H, 64]), op=ALU.mult),
                 rd=[bt3, bsg], wr=[bt3])
            rope(ts, t3[:, 0:n], [bt3], H, 2, tab[:, 64:128], tab[:, 128:192], btab, t3[:, 0:n], bt3)
            yield
            o_ap, i_ap, r_ap = final(t3[:, 0:n], rs)
            S.op("dve", lambda e: e.tensor_tensor(out=o_ap, in0=i_ap, in1=r_ap, op=ALU.mult),
                 rd=[bt3, brs], wr=[bdst])

        def transposes_bf16(stg_t, bstg, nblk, tix):
            pbank, bpbank = bk(tix)
            pv = bf16view(pbank)
            for j in range(nblk):
                S.op("pe", lambda e, j=j: e.transpose(out=pv[:, j, :], in_=stg_t[:, j * 128:(j + 1) * 128],
                                                      identity=ident[:]),
                     rd=[bstg, bconst], wr=[bpbank], inc=(j == nblk - 1))
            return pv, bpbank

        def attn_pass(nkt, scale, es, bgstep, hooks, first, nxt_es):
            Eslots = {}

            def scores(kt, es_):
                sl = kt % 2
                for j, e_ in enumerate(es_):
                    S.op("pe", lambda e, j=j, e_=e_, kt=kt, sl=sl: e.matmul(
                        out=PS[sl][:, j, :], lhsT=e_["kt"](kt), rhs=e_["q"], start=True, stop=True,
                        tile_position=(e_["row"], 0)),
                        rd=e_["rd"], wr=[bbank[sl][0], bbank[sl][1]], inc=(j == 1))

            def expo(kt):
                sl = kt % 2
                Et, bE = Ering.next()
                Eslots[kt] = (Et, bE)
                S.op("act", lambda e: e.activation(out=Et[:], in_=PS[sl][:].rearrange("p a b -> p (a b)"),
                                                   func=AF.Exp, scale=scale),
                     rd=[bbank[sl][0], bbank[sl][1]], wr=[bE])

            def pv(kt):
                Et, bE = Eslots.pop(kt)
                Ev = Et[:].rearrange("p (j q) -> p j q", j=2)
                for j, e_ in enumerate(es):
                    S.op("pe", lambda e, j=j, e_=e_, kt=kt, Ev=Ev: e.matmul(
                        out=PS[2][:, j, :], lhsT=e_["v"](kt), rhs=Ev[:, j, :],
                        start=(kt == 0), stop=(kt == nkt - 1)),
                        rd=[bE] + e_["rdv"], wr=[bbank[2][j]], inc=(j == 1))

            if first:
                scores(0, es)
                scores(1, es)
            for kt in range(nkt):
                expo(kt)
                if kt >= 2 and kt % 2 == 0 and hooks:
                    hooks.popleft()()
                if kt + 2 < nkt:
                    scores(kt + 2, es)
                elif nxt_es is not None:
                    scores(kt + 2 - nkt, nxt_es)
                pv(kt)
                bgstep()
            for j in range(2):
                S.op("dve", lambda e, j=j: e.tensor_copy(out=osb[0:65, j * 512:(j + 1) * 512], in_=PS[2][0:65, j, :]),
                     rd=[bbank[2][j]], wr=[bosb])

        (a1, a2, _), (ba1, ba2, _) = tts[2], btts[2]

        def post_T_g(tgt, btgt, g):
            pbank, bpbank = bk((3, 0))
            pv = pbank[:, 0:260].rearrange("p (a b) -> p a b", a=4)
            for qs in range(4):
                c0 = g * 512 + qs * 128
                S.op("pe", lambda e, qs=qs, c0=c0: e.transpose(
                    out=pv[:, qs, :], in_=osb[0:65, c0:c0 + 128], identity=identf[0:65, 0:65]),
                    rd=[bosb, bconst], wr=[bpbank], inc=(qs == 3))
            sm, bsm = smring.next()
            S.op("dve", lambda e: e.reciprocal(out=sm[:, 0:4], in_=pv[:, :, 64]), rd=[bpbank], wr=[bsm])
            tv = tgt[:, g * 256:(g + 1) * 256].rearrange("p (a b) -> p a b", a=4)
            S.op("dve", lambda e: e.tensor_tensor(out=tv, in0=pv[:, :, 0:64], in1=bc(sm[:, 0:4], 2, [128, 4, 64]),
                                                  op=ALU.mult), rd=[bpbank, bsm], wr=[btgt])
            return tv

        def post_plain(cp, cols):
            cc, bcc = ccs[cp], bccs[cp]

            def part(j):
                tv = post_T_g(a1, ba1, j)
                cv = cc[:, :, cols[j]:cols[j] + 64]
                S.op("dve", lambda e: e.tensor_tensor(out=cv, in0=tv, in1=cv, op=ALU.mult), rd=[ba1, bcc], wr=[bcc])
            return [lambda: part(0), lambda: part(1)]

        def post_diffA():
            return [lambda: post_T_g(hA, bhA, 0), lambda: post_T_g(hA, bhA, 1)]

        def post_diffB(cp, t):
            cc, bcc = ccs[cp], bccs[cp]
            st = {}

            def p1():
                post_T_g(a1, ba1, 1)
                S.op("dve", lambda e: e.scalar_tensor_tensor(out=a2[:], in0=a1[:], scalar=nlam[:, 0:1], in1=hA[:],
                                                             op0=ALU.mult, op1=ALU.add),
                     rd=[ba1, bhA, bnlam], wr=[ba2])
                S.op("dve", lambda e: e.tensor_tensor(out=a1[:], in0=a2[:], in1=a2[:], op=ALU.mult), rd=[ba2], wr=[ba1])
                sm, bsm = smring.next()
                S.op("dve", lambda e: e.reduce_sum(out=sm[:, 0:8], in_=a1[:].rearrange("p (a d) -> p a d", a=8),
                                                   axis=AX.X), rd=[ba1], wr=[bsm])
                st["sm"] = (sm, bsm)

            def p2():
                pass

            def p3():
                sm, bsm = st["sm"]
                rs, brs = rstd_of(sm[:, 0:8], 64, bsm, 8)
                d3 = a2[:].rearrange("p (a d) -> p a d", a=8)
                S.op("dve", lambda e: e.tensor_tensor(out=d3, in0=d3, in1=bc(rs, 2, [128, 8, 64]), op=ALU.mult),
                     rd=[ba2, brs], wr=[ba2])
                S.op("dve", lambda e: e.tensor_tensor(out=d3, in0=d3, in1=bc(sg2[:, 0:64], 1, [128, 8, 64]),
                                                      op=ALU.mult), rd=[ba2, bsg2], wr=[ba2])
                cv = cc[:, :, t * 128:(t + 1) * 128].rearrange("p q (g d) -> p g q d", g=2)
                dv = a2[:].rearrange("p (g q d) -> p g q d", g=2, q=4)
                S.op("dve", lambda e: e.tensor_tensor(out=cv, in0=dv, in1=cv, op=ALU.mult), rd=[ba2, bcc], wr=[bcc])
            return [lambda: post_T_g(a1, ba1, 0), p1, p2, p3]

        def attention(cp, bg, rate=1):
            QT, bQT = QTs[cp], bQTs[cp]

            nstep = [0]

            def bgstep():
                if bg is not None:
                    nstep[0] += 1
                    for _ in range(rate):
                        next(bg, None)
                    if nstep[0] % 2 == 0:
                        next(bg, None)
            passes = []

            def diff_pass(t, which):
                if True:
                    es = [dict(kt=lambda kt, rr_=rr_, t=t: KT[rr_:rr_ + 64, t, kt * 128:(kt + 1) * 128],
                               q=QT[rr_:rr_ + 64, 4 * which + t, :],
                               v=lambda kt, h=2 * t + rr_ // 64: Vflat[:, kt * 520 + h * 65:kt * 520 + h * 65 + 128],
                               row=rr_, rd=[bKT, bQT], rdv=[bV]) for rr_ in (0, 64)]
                    passes.append((32, 32 ** -0.5, es,
                                   post_diffA if which == 0 else (lambda t=t: post_diffB(cp, t))))

            def gqa_pass(j):
                es = [dict(kt=lambda kt, rr_=rr_: KT[rr_:rr_ + 64, 3, kt * 128:(kt + 1) * 128],
                           q=QT[rr_:rr_ + 64, 7 + j, :],
                           v=lambda kt, rr_=rr_: Vflat[:, kt * 520 + (6 + rr_ // 64) * 65:kt * 520 + (6 + rr_ // 64) * 65 + 128],
                           row=rr_, rd=[bKT, bQT], rdv=[bV]) for rr_ in (0, 64)]
                passes.append((32, 0.125, es, lambda j=j: post_plain(cp, (384 + j * 64, 384 + (j + 3) * 64))))

            def mem_pass(mb):
                es = [dict(kt=lambda kt, rr_=rr_, mb=mb: KmT[rr_:rr_ + 64, mb, kt * 128:(kt + 1) * 128],
                           q=QT[rr_:rr_ + 64, 3 + 7 * mb, :],
                           v=lambda kt, rr_=rr_, mb=mb: Vmflat[:, kt * 260 + (2 * mb + rr_ // 64) * 65:
                                                               kt * 260 + (2 * mb + rr_ // 64) * 65 + 128],
                           row=rr_, rd=[bKmT, bQT], rdv=[bVm]) for rr_ in (0, 64)]
                passes.append((2, 0.125, es, lambda mb=mb: post_plain(cp, (768 + mb * 128, 768 + mb * 128 + 64))))

            diff_pass(0, 0)
            mem_pass(0)
            diff_pass(0, 1)
            diff_pass(1, 0)
            mem_pass(1)
            diff_pass(1, 1)
            diff_pass(2, 0)
            diff_pass(2, 1)
            for j in range(3):
                gqa_pass(j)

            import collections
            hooks = collections.deque()
            for k, (nkt, scale, es, post) in enumerate(passes):
                nxt_es = passes[k + 1][2] if k + 1 < len(passes) else None
                if nkt < 8:
                    while hooks:
                        hooks.popleft()()
                attn_pass(nkt, scale, es, bgstep, hooks, k == 0, nxt_es)
                assert not hooks or nkt < 8
                hooks.extend(post())
            while hooks:
                hooks.popleft()()

        def g_silu(ts, pg, bpg, cp, ti, segs):
            (t1, t2, _), (bt1, bt2, _) = tts[ts], btts[ts]
            cc, bcc = ccs[cp], bccs[cp]
            S.op("act", lambda e: e.activation(out=t1[:], in_=pg, func=AF.Exp, scale=-1.0), rd=[bpg], wr=[bt1])
            S.op("dve", lambda e: e.tensor_copy(out=t2[:], in_=pg), rd=[bpg], wr=[bt2])
            yield
            yield
            S.op("dve", lambda e: e.tensor_scalar(out=t1[:], in0=t1[:], scalar1=1.0, scalar2=None, op0=ALU.add),
                 rd=[bt1], wr=[bt1])
            S.op("dve", lambda e: e.reciprocal(out=t1[:], in_=t1[:]), rd=[bt1], wr=[bt1])
            for c0, n, d0 in segs:
                S.op("dve", lambda e, c0=c0, n=n, d0=d0: e.tensor_tensor(
                    out=cc[:, ti, d0:d0 + n], in0=t2[:, c0:c0 + n], in1=t1[:, c0:c0 + n], op=ALU.mult),
                    rd=[bt1, bt2], wr=[bcc])
            yield
            yield

        def g_kv_tile(l, par, i, ld, wk, bwk, wv, bwv, ts, BS, xT_ap, bxT, hbr=None, sgr=None, alt="dve"):
            tab, btab = ld[2]
            yield from g_front(ld, gpre, bgpre, xT_ap, bxT, BS["T1"], ts, hbr, alt)
            pk, bpk = proj(xT_ap, bxT, wk, bwk, BS["PA"])
            yield
            yield
            yield
            sg_t, bstg = (sgr or stgring).next()
            rope(ts, pk[:, 0:384], [bpk], 12, 1, tab[:, 0:32], tab[:, 32:64], btab, sg_t[:, 0:384], bstg)
            yield
            yield from g_gqa_norm_rope(ts, pk[:, 384:512], [bpk], 2, sg[:, 128:192], tab, btab,
                                       lambda tn, rs, sg_t=sg_t: (sg_t[:, 384:512].rearrange("p (h d) -> p h d", h=2),
                                                                  tn.rearrange("p (h d) -> p h d", h=2),
                                                                  bc(rs, 2, [128, 2, 64])), bstg)
            yield
            yield
            yield
            yield
            pv, bpv = transposes_bf16(sg_t, bstg, 4, BS["T2"])
            yield
            yield
            yield
            pc, tq = i // 4, i % 4
            kt_t, bkt = ktring.next()
            if alt == "act":
                S.op("act", lambda e: e.copy(out=kt_t[:], in_=pv[:, 0:4, :]), rd=[bpv], wr=[bkt])
            else:
                S.op("dve", lambda e: e.tensor_copy(out=kt_t[:], in_=pv[:, 0:4, :]), rd=[bpv], wr=[bkt])
            S.dma("pool", ksnd[par][pc].rearrange("p (b k) -> p b k", b=4)[:, :, tq * 128:(tq + 1) * 128], kt_t[:],
                  rd=[bkt], wr=[bkvsK[par][i]])
            yield
            yield
            pvv, bpvv = proj(xT_ap, bxT, wv, bwv, BS["PB"])
            yield
            yield
            yield
            v_t, bvt = vring.next()
            if alt == "act":
                S.op("act", lambda e: e.copy(out=v_t[:, :, 0:64], in_=pvv.rearrange("p (h d) -> p h d", h=8)),
                     rd=[bpvv], wr=[bvt])
            else:
                S.op("dve", lambda e: e.tensor_copy(out=v_t[:, :, 0:64], in_=pvv.rearrange("p (h d) -> p h d", h=8)),
                     rd=[bpvv], wr=[bvt])
            S.dma("pool", vsnd[par][pc][:, tq * 520:(tq + 1) * 520], v_t[:].rearrange("p h d -> p (h d)"),
                  rd=[bvt], wr=[bkvsV[par][i]])
            if tq == 3:
                for kind, snd_, rcv_, bs_, br_ in (("k", ksnd, krcv, bkvsK, bkrcv), ("v", vsnd, vrcv, bkvsV, bvrcv)):
                    S.custom("pool", lambda e, snd_=snd_, rcv_=rcv_: e.collective_compute(
                        "AllGather", ALU.bypass, replica_groups=[[0, 1], [2, 3], [4, 5], [6, 7]],
                        ins=[snd_[par][pc]], outs=[rcv_[par][pc]]),
                        ccsems[l * 8 + pc * 2 + (kind == "v")], rd=bs_[par][pc * 4:pc * 4 + 4], wr=[br_[par][pc]])
            yield
            yield

        def kv_weights(l):
            return (wload([(w_in[l, :, 384:768], 0), (w_in[l, :, 1920:2048], 384)]),
                    wload([(w_in[l, :, 768:1152], 0), (w_in[l, :, 2048:2176], 384)]))

        def early_params(l):
            S.dma("sp", gpre[:], pre_n[l:l + 1, :].broadcast_to([128, D]), wr=[bgpre])
            S.dma("sp", sg[:], sgn[l:l + 1, :].broadcast_to([128, 192]), wr=[bsg])

        def g_early(l):
            early_params(l)
            yield

        def g_kv_bg(l, tiles, xp):
            (wk, bwk), (wv, bwv) = kv_weights(l)
            yield
            lds = {tiles[0]: xload(out[tiles[0] * 128:(tiles[0] + 1) * 128, :], [bxrow[tiles[0]]], tabo[tiles[0]])}
            for k, i in enumerate(tiles):
                if k + 1 < len(tiles):
                    j = tiles[k + 1]
                    lds[j] = xload(out[j * 128:(j + 1) * 128, :], [bxrow[j]], tabo[j])
                ld = lds[i]
                yield from g_kv_tile(l, l % 2, i, ld, wk, bwk, wv, bwv, 0, BS_BG[0], xTcs[xp][:, i % TPC],
                                     bxTs[xp][i % TPC])

        def reload_piece(par, pc):
            for r in range(2):
                S.dma("sp", KT[:, :, r * 2048 + pc * 512:r * 2048 + (pc + 1) * 512],
                      krcv[par][pc][r * 128:(r + 1) * 128, :].rearrange("p (b k) -> p b k", b=4),
                      rd=[bkrcv[par][pc]], wr=[bKT])
                S.dma("sp", V[:, r * 16 + pc * 4:r * 16 + pc * 4 + 4].rearrange("p t h d -> p t (h d)"),
                      vrcv[par][pc][r * 128:(r + 1) * 128, :].rearrange("p (t f) -> p t f", t=4),
                      rd=[bvrcv[par][pc]], wr=[bV])

        def kv_seq4(l, tiles):
            xsrc = xown if l == 0 else out
            (wk, bwk), (wv, bwv) = kv_weights(l)
            OFF = 12

            def start(k):
                i = tiles[k]
                s_ = k % 4
                rdx = [bxrow[i]] if l > 0 else []
                ld = xload(xsrc[i * 128:(i + 1) * 128, :], rdx, tabo[i], xs_kv, tab_kv)
                if k >= 7 and (k - 7) % 4 == 0:
                    reload_piece(l % 2, tiles[k - 7] // 4)
                return g_kv_tile(l, l % 2, i, ld, wk, bwk, wv, bwv, s_, BS_KV4[s_],
                                 xTcs[s_ % 2][:, s_ // 2 + 2 * ((k // 4) % 2)],
                                 bxTs[s_ % 2][s_ // 2 + 2 * ((k // 4) % 2)], hb_kv, stg_kv, "act")
            active = [None] * 4
            nxt, step = 0, 0
            while nxt < len(tiles) or any(g is not None for g in active):
                s_ = nxt % 4
                if nxt < len(tiles) and active[s_] is None and step >= nxt * OFF:
                    active[s_] = start(nxt)
                    nxt += 1
                for k_ in range(4):
                    if active[k_] is not None:
                        try:
                            next(active[k_])
                        except StopIteration:
                            active[k_] = None
                step += 1

        def kv_seq(l, tiles):
            xsrc = xown if l == 0 else out
            (wk, bwk), (wv, bwv) = kv_weights(l)

            def xl(i):
                rdx = [bxrow[i]] if l > 0 else []
                return xload(xsrc[i * 128:(i + 1) * 128, :], rdx, tabo[i])
            lds = {tiles[0]: xl(tiles[0]), tiles[1]: xl(tiles[1])}
            for k0 in range(0, len(tiles), 2):
                i0, i1 = tiles[k0], tiles[k0 + 1]
                if k0 > 0:
                    lds[i1] = xl(i1)
                if k0 + 2 < len(tiles):
                    lds[tiles[k0 + 2]] = xl(tiles[k0 + 2])
                gens = [g_kv_tile(l, l % 2, i, lds[i], wk, bwk, wv, bwv, i % 2, BS_SEQ[i % 2],
                                  xTcs[i % 2][:, (i // 2) % TPC], bxTs[i % 2][(i // 2) % TPC]) for i in (i0, i1)]
                drain(rr(*gens))

        def g_q_tile(gidx, cp, ti, tab_src, w, bw, ts, BS):
            tabld = None
            if tab_src is not None:
                tabld = tabring.next()
                S.dma("sp", tabld[0], tab_src, wr=[tabld[1]])
            xT_ap, bxT = xTcs[cp][:, ti], bxTs[cp][ti]
            QT, bQT = QTs[cp], bQTs[cp]
            pq, bpq = proj(xT_ap, bxT, w, bw, BS["PA"])
            yield
            yield
            yield
            if gidx == 0:
                tab, btab = tabld
                (t1, t2, _), (bt1, bt2, _) = tts[ts], btts[ts]
                sA, sB = stgQ[ts]
                bsq_ = bstgQ[ts]
                rope(ts, pq[:, 0:384], [bpq], 12, 1, tab[:, 0:32], tab[:, 32:64], btab, None, None)
                S.op("dve", lambda e: e.tensor_copy(out=sA[:, 384:512], in_=pq[:, 384:512]), rd=[bpq], wr=[bsq_])
                v3 = lambda ap: ap.rearrange("p (h w d) -> p h w d", h=6, w=2)
                for w_, sX in ((0, sA), (1, sB)):
                    S.op("dve", lambda e, w_=w_, sX=sX: e.tensor_tensor(
                        out=v3(sX[:, 0:384])[:, :, w_, :], in0=v3(t1[:, 0:384])[:, :, w_, :],
                        in1=v3(t2[:, 0:384])[:, :, w_, :], op=ALU.add), rd=[bt1, bt2], wr=[bsq_])
                yield
                yield
                yield
                yield
                pbank, bpbank = bk(BS["T2"])
                pv = bf16view(pbank)
                for k_, (sX, j) in enumerate([(sA, 0), (sA, 1), (sA, 2), (sA, 3), (sB, 0), (sB, 1), (sB, 2)]):
                    S.op("pe", lambda e, k_=k_, sX=sX, j=j: e.transpose(
                        out=pv[:, k_, :], in_=sX[:, j * 128:(j + 1) * 128], identity=ident[:]),
                        rd=[bsq_, bconst], wr=[bpbank], inc=(k_ == 6))
                yield
                yield
                S.op("dve", lambda e: e.tensor_copy(out=QT[:, 0:7, ti * 128:(ti + 1) * 128], in_=pv[:, 0:7, :]),
                     rd=[bpbank], wr=[bQT])
                yield
                yield
            elif gidx == 1:
                tab, btab = tabld
                sg_t, bstg = stgring.next()
                S.op("dve", lambda e: e.tensor_copy(out=sg_t[:, 384:512], in_=pq[:, 384:512]), rd=[bpq], wr=[bstg])
                yield from g_gqa_norm_rope(ts, pq[:, 0:384], [bpq], 6, sg[:, 64:128], tab, btab,
                                           lambda tn, rs, sg_t=sg_t: (
                                               sg_t[:, 0:384].rearrange("p (j h d) -> p h j d", j=3, h=2),
                                               tn.rearrange("p (h j d) -> p h j d", h=2, j=3),
                                               bc(rs.rearrange("p (h j) -> p h j", h=2), 3, [128, 2, 3, 64])), bstg)
                yield
                yield
                yield
                yield
                pv, bpv = transposes_bf16(sg_t, bstg, 4, BS["T2"])
                yield
                yield
                S.op("dve", lambda e: e.tensor_copy(out=QT[:, 7:11, ti * 128:(ti + 1) * 128], in_=pv[:, 0:4, :]),
                     rd=[bpv], wr=[bQT])
                yield
                yield
            else:
                segs = [(0, 384, 0), (384, 128, 768)] if gidx == 2 else [(0, 384, 384), (384, 128, 896)]
                yield from g_silu(ts, pq, bpq, cp, ti, segs)
            yield

        def g_phaseQ(l, c, BSs):
            cp = c % 2
            xsrc = xown if l == 0 else out

            def xl(i, with_tab):
                rdx = [bxrow[i]] if l > 0 else []
                return xload(xsrc[i * 128:(i + 1) * 128, :], rdx, tabo[i] if with_tab else None)
            groups = [
                [(w_in[l, :, 0:384], 0), (w_in[l, :, 2560:2688], 384)],
                [(w_in[l, :, 1536:1920], 0), (w_in[l, :, 2688:2816], 384)],
                [(w_in[l, :, 1152:1536], 0), (w_in[l, :, 2816:2944], 384)],
                [(w_in[l, :, 2176:2560], 0), (w_in[l, :, 2944:3072], 384)],
            ]
            nxtw = wload(groups[0])
            lds = [xl(c * TPC, False), xl(c * TPC + 1, False)]
            for ti in range(TPC):
                ld = lds[ti]
                if ti + 2 < TPC:
                    lds.append(xl(c * TPC + ti + 2, False))
                yield from g_front(ld, gpre, bgpre, xTcs[cp][:, ti], bxTs[cp][ti], BSs[ti % 2]["T1"], ti % 2)
            for gidx, parts in enumerate(groups):
                w, bw = nxtw
                if gidx + 1 < len(groups):
                    nxtw = wload(groups[gidx + 1])
                gens = []
                for ti in range(TPC):
                    gens.append(g_q_tile(gidx, cp, ti, tabo[c * TPC + ti] if gidx in (0, 1) else None,
                                         w, bw, ti % 2, BSs[ti % 2]))
                    if BSs[0] is BSs[1]:
                        yield from gens[ti]
                    elif ti % 2 == 1:
                        yield from rr(gens[ti - 1], gens[ti])

        def g_c_tile(l, c, ti, wo, BS, ts, yb=None):
            ysb, bysb = yb if yb is not None else (ysb_main, bysb_main)
            cp = c % 2
            cc, bcc = ccs[cp], bccs[cp]
            xsrc = xown if l == 0 else out
            i = c * TPC + ti
            rdx = [bxrow[i]] if l > 0 else []
            xs, bxs, _ = xload(xsrc[i * 128:(i + 1) * 128, :], rdx)
            tb, btb = bk(BS["T1"])
            pv = bf16view(tb)
            for kc in range(8):
                S.op("pe", lambda e, kc=kc: e.transpose(
                    out=pv[:, kc, :], in_=cc[:, ti, kc * 128:(kc + 1) * 128], identity=ident[:]),
                    rd=[bcc, bconst], wr=[btb], inc=(kc == 7))
            yield
            yield
            xT_ap, bxT = xTcs[cp][:, ti], bxTs[cp][ti]
            S.op("dve", lambda e: e.tensor_copy(out=xT_ap, in_=pv), rd=[btb], wr=[bxT])
            yield
            yield
            for g in range(2):
                py, bpy = proj(xT_ap, bxT, wo[g][0], wo[g][1], BS["PA"] if g == 0 else BS["T2"])
                yield
                yield
                S.op("dve", lambda e, g=g, py=py: e.tensor_copy(out=ysb[:, g * 512:(g + 1) * 512], in_=py),
                     rd=[bpy], wr=[bysb])
                yield
                yield
            sqj, bsq = TT[ts], [btts[ts][0], btts[ts][1]]
            S.op("dve", lambda e: e.tensor_tensor(out=sqj[:], in0=ysb[:], in1=ysb[:], op=ALU.mult),
                 rd=[bysb], wr=bsq)
            sm, bsm = smring.next()
            S.op("dve", lambda e: e.reduce_sum(out=sm[:, 0:1], in_=sqj[:], axis=AX.X), rd=bsq, wr=[bsm])
            yield
            yield
            yield
            yield
            rs, brs = rstd_of(sm[:, 0:1], D, bsm, 1)
            yield
            yield
            S.op("dve", lambda e: e.scalar_tensor_tensor(out=ysb[:], in0=ysb[:], scalar=rs, in1=gpost[:],
                                                         op0=ALU.mult, op1=ALU.mult),
                 rd=[bysb, brs, bgpost], wr=[bysb])
            S.op("dve", lambda e: e.tensor_tensor(out=xs[:], in0=xs[:], in1=ysb[:], op=ALU.add),
                 rd=[bxs, bysb], wr=[bxs])
            S.dma("pool", out[i * 128:(i + 1) * 128, :], xs[:], rd=[bxs], wr=[bxrow[i]])
            yield

        def load_wo(l):
            return [wload([(w_out[l, :, 0:512], 0)]), wload([(w_out[l, :, 512:1024], 0)])]

        def g_load_wo(l, box):
            box.append(load_wo(l))
            yield

        def g_phaseC(l, c, BSs, wo):
            if BSs[0] is BSs[1]:
                for ti in range(TPC):
                    yield from g_c_tile(l, c, ti, wo, BSs[ti % 2], ti % 2)
            else:
                ybs = [(ysb_main, bysb_main), (osb, bosb)]
                for t0 in range(0, TPC, 2):
                    yield from rr(*[g_c_tile(l, c, ti, wo, BSs[ti % 2], ti % 2, ybs[ti % 2]) for ti in (t0, t0 + 1)])

        for l in range(nl):
            xsrc = xown if l == 0 else out
            par = l % 2
            if l == 0:
                early_params(0)
            kv_seq4(l, list(range(NBG_KV if l > 0 else 0, NT_OWN)))

            S.dma("sp", gmem[:], mem_n[l:l + 1, :].broadcast_to([128, D]), wr=[bgmem])
            S.dma("sp", gpost[:], post_n[l:l + 1, :].broadcast_to([128, D]), wr=[bgpost])
            S.dma("sp", dl[:], dlam[l:l + 1, :].broadcast_to([128, 128]), wr=[bdl])
            S.dma("sp", lin_t[:], lini[l:l + 1, :].broadcast_to([128, 2]), wr=[blin])
            S.op("dve", lambda e: e.tensor_tensor(out=a1[:, 0:32], in0=dl[:, 0:32], in1=dl[:, 32:64], op=ALU.mult),
                 rd=[bdl], wr=[ba1])
            S.op("dve", lambda e: e.tensor_tensor(out=a1[:, 32:64], in0=dl[:, 64:96], in1=dl[:, 96:128], op=ALU.mult),
                 rd=[bdl], wr=[ba1])
            sm, bsm = smring.next()
            S.op("dve", lambda e, sm=sm: e.reduce_sum(out=sm[:, 0:2], in_=a1[:, 0:64].rearrange("p (a b) -> p a b", a=2),
                                                      axis=AX.X), rd=[ba1], wr=[bsm])
            S.op("act", lambda e, sm=sm: e.activation(out=sm[:, 2:4], in_=sm[:, 0:2], func=AF.Exp), rd=[bsm], wr=[bsm])
            S.op("dve", lambda e, sm=sm: e.tensor_tensor(out=nlam[:], in0=sm[:, 3:4], in1=sm[:, 2:3], op=ALU.subtract),
                 rd=[bsm], wr=[bnlam])
            S.op("dve", lambda e: e.tensor_scalar(out=nlam[:], in0=nlam[:], scalar1=lin_t[:, 0:1], scalar2=None,
                                                  op0=ALU.add), rd=[bnlam, blin], wr=[bnlam])
            S.op("dve", lambda e: e.tensor_scalar(out=sg2[:], in0=sg[:, 0:64], scalar1=lin_t[:, 1:2], scalar2=None,
                                                  op0=ALU.mult), rd=[bsg, blin], wr=[bsg2])

            wm, bwm = wload([(w_mkv[l], 0)])
            for mt in range(2):
                ld = xload(mem[mt * 128:(mt + 1) * 128, :], [])
                xT_ap, bxT = xTcs[1][:, mt], bxTs[1][mt]
                drain(g_front(ld, gmem, bgmem, xT_ap, bxT, (3, 0), 0))
                pm, bpm = proj(xT_ap, bxT, wm, bwm, (2, mt))
                sg_t, bstg = stgring.next()
                S.op("dve", lambda e, sg_t=sg_t, pm=pm: e.tensor_copy(out=sg_t[:, 0:256], in_=pm[:, 0:256]),
                     rd=[bpm], wr=[bstg])
                pv, bpv = transposes_bf16(sg_t, bstg, 2, (3, 1))
                S.op("dve", lambda e, pv=pv, mt=mt: e.tensor_copy(out=KmT[:, :, mt * 128:(mt + 1) * 128],
                                                                  in_=pv[:, 0:2, :]), rd=[bpv], wr=[bKmT])
                S.op("dve", lambda e, pm=pm, mt=mt: e.tensor_copy(
                    out=Vm[:, mt, :, 0:64], in_=pm[:, 256:512].rearrange("p (h d) -> p h d", h=4)),
                    rd=[bpm], wr=[bVm])

            for pc in range(3, 4):
                for r in range(2):
                    S.dma("sp", KT[:, :, r * 2048 + pc * 512:r * 2048 + (pc + 1) * 512],
                          krcv[par][pc][r * 128:(r + 1) * 128, :].rearrange("p (b k) -> p b k", b=4),
                          rd=[bkrcv[par][pc]], wr=[bKT])
                    S.dma("sp", V[:, r * 16 + pc * 4:r * 16 + pc * 4 + 4].rearrange("p t h d -> p t (h d)"),
                          vrcv[par][pc][r * 128:(r + 1) * 128, :].rearrange("p (t f) -> p t f", t=4),
                          rd=[bvrcv[par][pc]], wr=[bV])

            import os
            if l == 0:
                drain(g_phaseQ(l, 0, BS_SEQ))
            if "seq" in os.environ.get("KDBG", ""):
                for c in range(NCH):
                    if c > 0:
                        drain(g_phaseQ(l, c, BS_SEQ))
                    attention(c % 2, None)
                    drain(g_phaseC(l, c, BS_SEQ, load_wo(l)))
                continue
            DBG = os.environ.get("KDBG", "")
            wos = []
            for c in range(NCH):
                nxt = (c == NCH - 1 and l + 1 < nl)
                bg = chain(g_phaseC(l, c - 1, BS_BG, wos[c - 1]) if c > 0 else None,
                           g_phaseQ(l, c + 1, BS_BG) if c + 1 < NCH else None,
                           g_early(l + 1) if nxt else None,
                           g_phaseQ(l + 1, 0, BS_BG) if nxt else None,
                           g_kv_bg(l + 1, list(range(NBG_KV)), (c - 1) % 2) if (nxt and NBG_KV) else None,
                           g_load_wo(l, wos))
                attention(c % 2, bg, 1)
                drain(bg)
            drain(g_phaseC(l, NCH - 1, BS_SEQ, wos[NCH - 1]))

        S.wait_all("pool", bxrow)
        S.emit(block)
    return nc


def _rope_tab(pos, dim):
    inv = (10000.0 ** (-np.arange(0, dim, 2, dtype=np.float32) / np.float32(dim))).astype(np.float32)
    ang = pos.astype(np.float32)[:, None] * inv[None, :]
    return np.cos(ang).astype(np.float32), np.sin(ang).astype(np.float32)


def _tables():
    t = np.arange(4096)
    lc, ls = _rope_tab(t, 32)
    rc, rs = _rope_tab(t // 64, 32)
    cc_, cs = _rope_tab(t % 64, 32)
    tab = np.concatenate([lc, lc, -ls, ls, rc, rc, cc_, cc_, -rs, rs, -cs, cs], axis=1).astype(np.float32)
    return tab.reshape(32, 128, 192)


_CACHE = {}


def _get_nc(nl):
    if nl not in _CACHE:
        _CACHE[nl] = build(nl)
    return _CACHE[nl]


FUSED = True


def _in_maps(x_cur, mem, layers, p, tab):
    nl = len(layers)
    f = lambda a: np.ascontiguousarray(a, dtype=np.float32)
    sl = lambda a: f(a[layers])
    lin = np.array([[-(0.8 - 0.6 * math.exp(-0.3 * l)), 1.0 - (0.8 - 0.6 * math.exp(-0.3 * l))] for l in layers],
                   dtype=np.float32)
    shared = {
        "w_in": sl(p["w_in"]), "w_mkv": sl(p["w_mem_kv"]), "w_out": sl(p["w_out"]),
        "pre_n": sl(p["pre_norm"]), "mem_n": sl(p["mem_norm"]), "post_n": sl(p["post_norm"]),
        "dlam": sl(p["diff_lambda"]).reshape(nl, 128),
        "sgn": f(np.concatenate([p["diff_subln"][layers], p["gqa_q_norm"][layers], p["gqa_k_norm"][layers]], axis=1)),
        "lini": lin,
        "ident": np.eye(128).astype(ml_dtypes.bfloat16),
        "identf": np.eye(128, dtype=np.float32),
    }
    maps = []
    for c in range(8):
        b, r = c // 2, c % 2
        m = dict(shared)
        m["xown"] = f(x_cur[b, r * S_OWN:(r + 1) * S_OWN])
        m["mem"] = f(mem[b])
        m["tabo"] = f(tab[r * 16:(r + 1) * 16])
        maps.append(m)
    return maps


def _run(x_cur, mem, layers, p, tab):
    nc = _get_nc(len(layers))
    res = run_bass_kernel_spmd(nc, _in_maps(x_cur, mem, layers, p, tab), core_ids=list(range(8)))
    y = np.empty_like(x_cur)
    for c in range(8):
        b, r = c // 2, c % 2
        y[b, r * S_OWN:(r + 1) * S_OWN] = res.results[c]["out"]
    return y


def kernel(x, mem, pre_norm, w_in, diff_lambda, diff_subln, gqa_q_norm, gqa_k_norm,
           mem_norm, w_mem_kv, w_out, post_norm, _layers=None):
    p = dict(pre_norm=np.asarray(pre_norm), w_in=np.asarray(w_in), diff_lambda=np.asarray(diff_lambda),
             diff_subln=np.asarray(diff_subln), gqa_q_norm=np.asarray(gqa_q_norm),
             gqa_k_norm=np.asarray(gqa_k_norm), mem_norm=np.asarray(mem_norm), w_mem_kv=np.asarray(w_mem_kv),
             w_out=np.asarray(w_out), post_norm=np.asarray(post_norm))
    x = np.asarray(x, dtype=np.float32)
    mem = np.asarray(mem, dtype=np.float32)
    tab = _tables()
    depth = p["w_in"].shape[0] if _layers is None else _layers
    if FUSED:
        return _run(x, mem, list(range(depth)), p, tab)
    for l in range(depth):
        x = _run(x, mem, [l], p, tab)
    return x
```

```python
import contextlib
import math
import numpy as np
import ml_dtypes
import concourse.bass as bass
import concourse.mybir as mybir
from concourse.bass_utils import run_bass_kernel_spmd

F32 = mybir.dt.float32
BF16 = mybir.dt.bfloat16
ALU = mybir.AluOpType
AF = mybir.ActivationFunctionType
AX = mybir.AxisListType

ENGS = ("pe", "act", "dve", "pool", "sp")
D = 1024
S_OWN = 2048
NT_OWN = 16
NBG_KV = 0
CH = 512
NCH = S_OWN // CH
TPC = CH // 128
FK_KT = 4 * 2048
FK_V = 16 * 8 * 65
FK = FK_KT + FK_V
EPS = 1e-6


class Buf:
    __slots__ = ("name", "w", "r", "excl")

    def __init__(self, name, excl=False):
        self.name = name
        self.w = None
        self.r = {}
        self.excl = excl


class Sched:
    def __init__(self, sems, dma_sems, n_sw):
        self.sem = sems
        self.cnt = {e: 0 for e in ENGS}
        self.prog = {e: [] for e in ENGS}
        self.seen = {e: {} for e in ENGS}
        self.dma_sems = dma_sems
        self.dma_cnt = [0] * len(dma_sems)
        self.n_sw = n_sw
        self.dma_rr = {True: 0, False: 0}

    def _need(self, eng, tickets):
        waits = {}
        for t in tickets:
            if t is None:
                continue
            key, sh, val = t
            if key == eng and eng == "pe":
                continue
            if self.seen[eng].get(key, 0) >= val:
                continue
            if waits.get(key, (None, 0))[1] < val:
                waits[key] = (sh, val)
        for key, (sh, val) in waits.items():
            self.seen[eng][key] = val
            self.prog[eng].append(lambda e, sh=sh, val=val: e.wait_ge(sh, val))

    def _deps(self, eng, rd, wr):
        tickets = []
        for b in rd:
            tickets.append(b.w)
            if b.excl:
                tickets.extend(t for k, t in b.r.items() if k != eng)
        for b in wr:
            tickets.append(b.w)
            tickets.extend(b.r.values())
        return tickets

    def _mark(self, t, rd, wr):
        key = t[0]
        for b in rd:
            old = b.r.get(key)
            if old is None or old[2] < t[2]:
                b.r[key] = t
        for b in wr:
            b.w = t
            b.r = {}

    def op(self, eng, fn, rd=(), wr=(), inc=True):
        self._need(eng, self._deps(eng, rd, wr))
        sh = self.sem[eng]
        if inc:
            self.cnt[eng] += 1
            val = self.cnt[eng]
            self.prog[eng].append(lambda e, fn=fn, sh=sh: fn(e).then_inc(sh, 1))
        else:
            val = self.cnt[eng] + 1
            self.prog[eng].append(lambda e, fn=fn: fn(e))
        t = (eng, sh, val)
        self._mark(t, rd, wr)
        return t

    def dma(self, eng, out, in_, rd=(), wr=(), **kw):
        sw = eng == "pool"
        k = self.dma_rr[sw]
        n = self.n_sw if sw else len(self.dma_sems) - self.n_sw
        self.dma_rr[sw] = (k + 1) % n
        i = k if sw else self.n_sw + k
        sh = self.dma_sems[i]
        key = ("d", i)
        tickets = self._deps(key, rd, wr)
        if self.dma_cnt[i]:
            tickets.append((key, sh, self.dma_cnt[i]))
        self._need(eng, tickets)
        self.dma_cnt[i] += 16
        val = self.dma_cnt[i]
        self.prog[eng].append(
            lambda e, out=out, in_=in_, kw=kw, sh=sh: e.dma_start(out=out, in_=in_, **kw).then_inc(sh, 16))
        t = (key, sh, val)
        self._mark(t, rd, wr)
        return t

    def custom(self, eng, fn, sh, rd=(), wr=()):
        self._need(eng, self._deps(("c", id(sh)), rd, wr))
        self.prog[eng].append(lambda e, fn=fn, sh=sh: fn(e).then_inc(sh, 1))
        t = (("c", id(sh)), sh, 1)
        self._mark(t, rd, wr)
        return t

    def wait_all(self, eng, bufs):
        self._need(eng, [b.w for b in bufs])

    def emit(self, block):
        hooks = {"pe": block.tensor, "act": block.scalar, "dve": block.vector,
                 "pool": block.gpsimd, "sp": block.sync}
        for eng in ENGS:
            prog = self.prog[eng]

            def body(e, prog=prog):
                for f in prog:
                    f(e)
            hooks[eng](body)


class Ring:
    def __init__(self, items):
        self.items = items
        self.i = 0

    def next(self):
        it = self.items[self.i % len(self.items)]
        self.i += 1
        return it


def bc(ap, axis, shape):
    return ap.unsqueeze(axis).broadcast_to(list(shape))


def drain(gen):
    for _ in gen:
        pass


def rr(*gens):
    gens = [g for g in gens if g is not None]
    while gens:
        for g in list(gens):
            try:
                next(g)
            except StopIteration:
                gens.remove(g)
        yield


def chain(*gens):
    for g in gens:
        if g is not None:
            yield from g


def build(nl):
    nc = bass.Bass("TRN2", target_bir_lowering=False)
    dt_ = nc.dram_tensor
    xown = dt_("xown", [S_OWN, D], F32, kind="ExternalInput").ap()
    mem = dt_("mem", [256, D], F32, kind="ExternalInput").ap()
    w_in = dt_("w_in", [nl, D, 3072], F32, kind="ExternalInput").ap()
    w_mkv = dt_("w_mkv", [nl, D, 512], F32, kind="ExternalInput").ap()
    w_out = dt_("w_out", [nl, D, D], F32, kind="ExternalInput").ap()
    pre_n = dt_("pre_n", [nl, D], F32, kind="ExternalInput").ap()
    mem_n = dt_("mem_n", [nl, D], F32, kind="ExternalInput").ap()
    post_n = dt_("post_n", [nl, D], F32, kind="ExternalInput").ap()
    dlam = dt_("dlam", [nl, 128], F32, kind="ExternalInput").ap()
    sgn = dt_("sgn", [nl, 192], F32, kind="ExternalInput").ap()
    lini = dt_("lini", [nl, 2], F32, kind="ExternalInput").ap()
    tabo = dt_("tabo", [NT_OWN, 128, 192], F32, kind="ExternalInput").ap()
    ident_d = dt_("ident", [128, 128], BF16, kind="ExternalInput").ap()
    identf_d = dt_("identf", [128, 128], F32, kind="ExternalInput").ap()
    out = dt_("out", [S_OWN, D], F32, kind="ExternalOutput").ap()
    NPAR = min(2, nl)
    ksnd = [[dt_(f"ksnd{i}_{p}", [128, 4 * 512], BF16).ap() for p in range(4)] for i in range(NPAR)]
    krcv = [[dt_(f"krcv{i}_{p}", [256, 4 * 512], BF16).ap() for p in range(4)] for i in range(NPAR)]
    vsnd = [[dt_(f"vsnd{i}_{p}", [128, 4 * 520], BF16).ap() for p in range(4)] for i in range(NPAR)]
    vrcv = [[dt_(f"vrcv{i}_{p}", [256, 4 * 520], BF16).ap() for p in range(4)] for i in range(NPAR)]

    with contextlib.ExitStack() as st:
        E = st.enter_context

        def sb(name, shape, dt):
            return E(nc.sbuf_tensor(name, shape, dt))

        KT = sb("KT", [128, 4, 4096], BF16)
        Vflat = sb("V", [128, 32 * 520 + 64], BF16)
        V = Vflat[:, 0:32 * 520].rearrange("p (t h d) -> p t h d", t=32, h=8)
        QTs = [sb(f"QT{i}", [128, 11, CH], BF16) for i in range(2)]
        ccs = [sb(f"cc{i}", [128, TPC, D], BF16) for i in range(2)]
        xTcs = [sb(f"xTc{i}", [128, TPC, 8, 128], BF16) for i in range(2)]
        Wr = [sb(f"W{i}", [128, 8, 512], BF16) for i in range(2)]
        Er = [sb(f"E{i}", [128, 1024], BF16) for i in range(3)]
        xsr = [sb(f"xs{i}", [128, D], F32) for i in range(3)]
        hbr = [sb(f"hb{i}", [128, D], BF16) for i in range(2)]
        TT = [sb(f"tt{i}", [128, 1024], F32) for i in range(3)]
        T3 = [sb(f"t3_{i}", [128, 384], F32) for i in range(2)]
        tts = [[TT[i][:, 0:512], TT[i][:, 512:1024], T3[i][:] if i < 2 else None] for i in range(3)]
        TT = list(TT)
        hA = sb("hA", [128, 512], F32)
        stgQ = [[sb(f"stgQ{i}{k}", [128, 512], BF16) for k in range(2)] for i in range(2)]
        stg = [sb(f"stg{i}", [128, 512], BF16) for i in range(2)]
        ktst = [sb(f"ktst{i}", [128, 4, 128], BF16) for i in range(2)]
        vst = [sb(f"vst{i}", [128, 8, 65], BF16) for i in range(2)]
        osb = sb("osb", [128, 1024], F32)
        tabr = [sb(f"tab{i}", [128, 192], F32) for i in range(3)]
        gpre = sb("gpre", [128, D], F32)
        gpost = sb("gpost", [128, D], F32)
        dl = sb("dl", [128, 128], F32)
        sg = sb("sg", [128, 192], F32)
        sg2 = sb("sg2", [128, 64], F32)
        lin_t = sb("lin_t", [128, 2], F32)
        small = [sb(f"sm{i}", [128, 16], F32) for i in range(16)]
        nlam = sb("nlam", [128, 1], F32)
        eps_t = sb("eps_t", [128, 1], F32)
        KmT = sb("KmT", [128, 2, 256], BF16)
        Vmflat = sb("Vm", [128, 2 * 260 + 64], BF16)
        Vm = Vmflat[:, 0:2 * 260].rearrange("p (t h d) -> p t h d", t=2, h=4)
        ysb = sb("ysb", [128, D], F32)
        gmem = ysb
        ident = sb("identsb", [128, 128], BF16)
        identf = sb("identfsb", [128, 128], F32)
        PS = [E(nc.psum_tensor(f"PS{i}", [128, 2, 512], F32)) for i in range(4)]
        sems = {e: E(nc.semaphore(f"s_{e}")) for e in ENGS}
        dsems = [E(nc.semaphore(f"d{i}")) for i in range(32)]
        ccsems = [E(nc.semaphore(f"cc{i}")) for i in range(nl * 8)]
        block = E(nc.Block())
        S = Sched(sems, dsems, 12)

        bKT, bV = Buf("KT"), Buf("V")
        bQTs = [Buf("QT0"), Buf("QT1")]
        bccs = [Buf("cc0"), Buf("cc1")]
        bxTs = [[Buf(f"xT{p}_{i}") for i in range(TPC)] for p in range(2)]
        Wring = Ring([(Wr[i], Buf(f"W{i}")) for i in range(2)])
        Ering = Ring([(Er[i], Buf(f"E{i}")) for i in range(3)])
        xsring = Ring([(xsr[i], Buf(f"xs{i}")) for i in range(3)])
        hbring = Ring([(hbr[i], Buf(f"hb{i}")) for i in range(2)])
        stgring = Ring([(stg[i], Buf(f"stg{i}")) for i in range(2)])
        bstgQ = [Buf("stgQ0"), Buf("stgQ1")]
        bhA = Buf("hA")
        ktring = Ring([(ktst[i], Buf(f"ktst{i}")) for i in range(2)])
        vring = Ring([(vst[i], Buf(f"vst{i}")) for i in range(2)])
        bosb = Buf("osb")
        tabring = Ring([(tabr[i][:], Buf(f"tab{i}")) for i in range(3)])
        smring = Ring([(small[i], Buf(f"sm{i}")) for i in range(16)])
        btts = [[Buf(f"t{k}_{i}") for k in range(3)] for i in range(3)]
        bgpre, bgpost, bdl, bsg, bsg2, blin, bnlam = (Buf(n) for n in
                                                      ["gpre", "gpost", "dl", "sg", "sg2", "lin", "nlam"])
        bconst = Buf("const")
        bKmT, bVm, bysb = Buf("KmT"), Buf("Vm"), Buf("ysb")
        bgmem = bysb
        ysb_main, bysb_main = ysb, bysb
        bank = [[PS[i][:, j, :] for j in range(2)] for i in range(4)]
        bbank = [[Buf(f"ps{i}{j}", excl=True) for j in range(2)] for i in range(4)]
        bxrow = [Buf(f"xrow{i}") for i in range(NT_OWN)]
        bkvsK = [[Buf(f"kvsK{i}_{t}") for t in range(NT_OWN)] for i in range(2)]
        bkvsV = [[Buf(f"kvsV{i}_{t}") for t in range(NT_OWN)] for i in range(2)]
        bkrcv = [[Buf(f"krcv{i}_{p}") for p in range(4)] for i in range(2)]
        bvrcv = [[Buf(f"vrcv{i}_{p}") for p in range(4)] for i in range(2)]

        tts[2][2] = hA[:, 0:384]
        btts[2][2] = bhA
        TT.append(osb)
        tts.append([osb[:, 0:512], osb[:, 512:1024], Er[0][:].bitcast(F32)[:, 0:384]])
        btts.append([bosb, bosb, Ering.items[0][1]])
        xs_kv = Ring([(xsr[0], xsring.items[0][1]), (xsr[1], xsring.items[1][1]), (xsr[2], xsring.items[2][1]),
                      (ysb, bysb)])
        hb_kv = Ring([hbring.items[0], hbring.items[1], (Er[1], Ering.items[1][1]), (Er[2], Ering.items[2][1])])
        stg_kv = Ring([stgring.items[0], stgring.items[1], (QTs[1][:, 0, :], bQTs[1]), (QTs[1][:, 1, :], bQTs[1])])
        tab_kv = Ring([tabring.items[0], tabring.items[1], tabring.items[2],
                       (ccs[1][:, 0, :].bitcast(F32)[:, 0:192], bccs[1])])
        BS_KV4 = [dict(T1=(i, 0), PA=(i, 1), PB=(i, 1), T2=(i, 0)) for i in range(4)]

        BS_SEQ = [dict(T1=(3, 0), PA=(2, 0), PB=(2, 1), T2=(3, 1)), dict(T1=(1, 0), PA=(0, 0), PB=(0, 1), T2=(1, 1))]
        BS_BG = [dict(T1=(3, 1), PA=(3, 1), PB=(3, 1), T2=(3, 1))] * 2

        def bk(ix):
            return bank[ix[0]][ix[1]], bbank[ix[0]][ix[1]]

        def bf16view(bkap):
            return bkap.bitcast(BF16).rearrange("p (a b) -> p a b", a=8)

        S.dma("sp", ident[:], ident_d, wr=[bconst])
        S.dma("sp", identf[:], identf_d, wr=[bconst])
        S.op("pool", lambda e: e.memset(eps_t[:], EPS), wr=[bconst])
        S.op("pool", lambda e: e.memset(Vflat[:], 1.0), wr=[bV])
        S.op("pool", lambda e: e.memset(Vmflat[:], 1.0), wr=[bVm])
        for i in range(2):
            S.op("pool", lambda e, i=i: e.memset(vst[i][:], 1.0), wr=[vring.items[i][1]])
            for k in range(2):
                S.op("pool", lambda e, i=i, k=k: e.memset(stgQ[i][k][:], 0.0), wr=[bstgQ[i]])

        def rstd_of(ss_ap, n, bss, cols):
            sm, bsm = smring.next()
            S.op("act", lambda e: e.activation(out=sm[:, 0:cols], in_=ss_ap, func=AF.Ln, scale=1.0 / n,
                                               bias=eps_t[:, 0:1]), rd=[bss, bconst], wr=[bsm])
            S.op("act", lambda e: e.activation(out=sm[:, 8:8 + cols], in_=sm[:, 0:cols], func=AF.Exp, scale=-0.5),
                 rd=[bsm], wr=[bsm])
            return sm[:, 8:8 + cols], bsm

        def xload(src_ap, rd_src, tab_src=None, xr=None, tr=None):
            xs, bxs = (xr or xsring).next()
            S.dma("sp", xs[:], src_ap, rd=rd_src, wr=[bxs])
            tb = None
            if tab_src is not None:
                tb = (tr or tabring).next()
                S.dma("sp", tb[0], tab_src, wr=[tb[1]])
            return xs, bxs, tb

        def g_front(ld, gain, bgain, dst_ap, bdst, tbank, ts, hbr=None, alt="dve"):
            xs, bxs = ld[0], ld[1]
            pbank, bpbank = bk(tbank)
            sqj, bsq = TT[ts], [btts[ts][0], btts[ts][1]]
            sm, bsm = smring.next()
            if alt == "act":
                S.op("act", lambda e: e.activation(out=sqj[:], in_=xs[:], func=AF.Square, accum_out=sm[:, 0:1]),
                     rd=[bxs], wr=bsq + [bsm])
            else:
                S.op("dve", lambda e: e.tensor_tensor(out=sqj[:], in0=xs[:], in1=xs[:], op=ALU.mult), rd=[bxs], wr=bsq)
                S.op("dve", lambda e: e.reduce_sum(out=sm[:, 0:1], in_=sqj[:], axis=AX.X), rd=bsq, wr=[bsm])
            yield
            yield
            yield
            yield
            rs, brs = rstd_of(sm[:, 0:1], D, bsm, 1)
            yield
            yield
            hb, bhb = (hbr or hbring).next()
            S.op("dve", lambda e: e.scalar_tensor_tensor(out=hb[:], in0=xs[:], scalar=rs, in1=gain[:],
                                                         op0=ALU.mult, op1=ALU.mult),
                 rd=[bxs, brs, bgain], wr=[bhb])
            yield
            yield
            yield
            yield
            pv = bf16view(pbank)
            for kc in range(8):
                S.op("pe", lambda e, kc=kc: e.transpose(out=pv[:, kc, :], in_=hb[:, kc * 128:(kc + 1) * 128],
                                                        identity=ident[:]),
                     rd=[bhb, bconst], wr=[bpbank], inc=(kc == 7))
            yield
            yield
            if alt == "act":
                S.op("act", lambda e: e.copy(out=dst_ap, in_=pv), rd=[bpbank], wr=[bdst])
            else:
                S.op("dve", lambda e: e.tensor_copy(out=dst_ap, in_=pv), rd=[bpbank], wr=[bdst])
            yield
            yield
            yield

        def wload(parts):
            w, bw = Wring.next()
            for src, c0 in parts:
                n = src.shape[1]
                S.dma("pool", w[:, :, c0:c0 + n], src.rearrange("(kc p) n -> p kc n", p=128), wr=[bw])
            return w, bw

        def proj(xT_ap, bx, w, bw, pix):
            pbank, bpbank = bk(pix)
            for kc in range(8):
                S.op("pe", lambda e, kc=kc: e.matmul(out=pbank, lhsT=xT_ap[:, kc, :], rhs=w[:, kc, :],
                                                     start=(kc == 0), stop=(kc == 7)),
                     rd=[bx, bw], wr=[bpbank], inc=(kc == 7))
            return pbank, bpbank

        def rope(ts, src, rd_src, G, inner, C, Sg, btab, dst, bdst):
            (t1, t2, _), (bt1, bt2, _) = tts[ts], btts[ts]
            n = G * inner * 32
            v4 = lambda ap: ap.rearrange("p (g i d) -> p g i d", g=G, i=inner)
            sv, t1v, t2v = v4(src), v4(t1[:, 0:n]), v4(t2[:, 0:n])
            Cv = bc(C.rearrange("p (i d) -> p i d", i=inner), 1, [128, G, inner, 32])
            Sv = Sg.rearrange("p (i d) -> p i d", i=inner)
            S.op("dve", lambda e: e.tensor_tensor(out=t1v, in0=sv, in1=Cv, op=ALU.mult), rd=rd_src + [btab], wr=[bt1])
            S.op("dve", lambda e: e.tensor_tensor(out=t2v[:, :, :, 0:16], in0=sv[:, :, :, 16:32],
                                                  in1=bc(Sv[:, :, 0:16], 1, [128, G, inner, 16]), op=ALU.mult),
                 rd=rd_src + [btab], wr=[bt2])
            S.op("dve", lambda e: e.tensor_tensor(out=t2v[:, :, :, 16:32], in0=sv[:, :, :, 0:16],
                                                  in1=bc(Sv[:, :, 16:32], 1, [128, G, inner, 16]), op=ALU.mult),
                 rd=rd_src + [btab], wr=[bt2])
            if dst is not None:
                S.op("dve", lambda e: e.tensor_tensor(out=dst, in0=t1[:, 0:n], in1=t2[:, 0:n], op=ALU.add),
                     rd=[bt1, bt2], wr=[bdst])

        def g_gqa_norm_rope(ts, src, rd_src, H, gain_ap, tab, btab, final, bdst):
            (t1, t2, t3), (bt1, bt2, bt3) = tts[ts], btts[ts]
            n = H * 64
            S.op("dve", lambda e: e.tensor_copy(out=t3[:, 0:n], in_=src), rd=rd_src, wr=[bt3])
            S.op("dve", lambda e: e.tensor_tensor(out=t1[:, 0:n], in0=t3[:, 0:n], in1=t3[:, 0:n], op=ALU.mult),
                 rd=[bt3], wr=[bt1])
            sm, bsm = smring.next()
            S.op("dve", lambda e: e.reduce_sum(out=sm[:, 0:H], in_=t1[:, 0:n].rearrange("p (h d) -> p h d", h=H),
                                               axis=AX.X), rd=[bt1], wr=[bsm])
            yield
            yield
            yield
            yield
            rs, brs = rstd_of(sm[:, 0:H], 64, bsm, H)
            S.op("dve", lambda e: e.tensor_tensor(out=t3[:, 0:n].rearrange("p (h d) -> p h d", h=H),
                                                  in0=t3[:, 0:n].rearrange("p (h d) -> p h d", h=H),
                                                  in1=bc(gain_ap, 1, [128, H, 64]), op=ALU.mult),
                 rd=[bt3, bsg], wr=[bt3])
            rope(ts, t3[:, 0:n], [bt3], H, 2, tab[:, 64:128], tab[:, 128:192], btab, t3[:, 0:n], bt3)
            yield
            o_ap, i_ap, r_ap = final(t3[:, 0:n], rs)
            S.op("dve", lambda e: e.tensor_tensor(out=o_ap, in0=i_ap, in1=r_ap, op=ALU.mult),
                 rd=[bt3, brs], wr=[bdst])

        def transposes_bf16(stg_t, bstg, nblk, tix):
            pbank, bpbank = bk(tix)
            pv = bf16view(pbank)
            for j in range(nblk):
                S.op("pe", lambda e, j=j: e.transpose(out=pv[:, j, :], in_=stg_t[:, j * 128:(j + 1) * 128],
                                                      identity=ident[:]),
                     rd=[bstg, bconst], wr=[bpbank], inc=(j == nblk - 1))
            return pv, bpbank

        def attn_pass(nkt, scale, es, bgstep, hooks, first, nxt_es):
            Eslots = {}

            def scores(kt, es_):
                sl = kt % 2
                for j, e_ in enumerate(es_):
                    S.op("pe", lambda e, j=j, e_=e_, kt=kt, sl=sl: e.matmul(
                        out=PS[sl][:, j, :], lhsT=e_["kt"](kt), rhs=e_["q"], start=True, stop=True,
                        tile_position=(e_["row"], 0)),
                        rd=e_["rd"], wr=[bbank[sl][0], bbank[sl][1]], inc=(j == 1))

            def expo(kt):
                sl = kt % 2
                Et, bE = Ering.next()
                Eslots[kt] = (Et, bE)
                S.op("act", lambda e: e.activation(out=Et[:], in_=PS[sl][:].rearrange("p a b -> p (a b)"),
                                                   func=AF.Exp, scale=scale),
                     rd=[bbank[sl][0], bbank[sl][1]], wr=[bE])

            def pv(kt):
                Et, bE = Eslots.pop(kt)
                Ev = Et[:].rearrange("p (j q) -> p j q", j=2)
                for j, e_ in enumerate(es):
                    S.op("pe", lambda e, j=j, e_=e_, kt=kt, Ev=Ev: e.matmul(
                        out=PS[2][:, j, :], lhsT=e_["v"](kt), rhs=Ev[:, j, :],
                        start=(kt == 0), stop=(kt == nkt - 1)),
                        rd=[bE] + e_["rdv"], wr=[bbank[2][j]], inc=(j == 1))

            if first:
                scores(0, es)
                scores(1, es)
            for kt in range(nkt):
                expo(kt)
                if kt >= 2 and kt % 2 == 0 and hooks:
                    hooks.popleft()()
                if kt + 2 < nkt:
                    scores(kt + 2, es)
                elif nxt_es is not None:
                    scores(kt + 2 - nkt, nxt_es)
                pv(kt)
                bgstep()
            for j in range(2):
                S.op("dve", lambda e, j=j: e.tensor_copy(out=osb[0:65, j * 512:(j + 1) * 512], in_=PS[2][0:65, j, :]),
                     rd=[bbank[2][j]], wr=[bosb])

        (a1, a2, _), (ba1, ba2, _) = tts[2], btts[2]

        def post_T_g(tgt, btgt, g):
            pbank, bpbank = bk((3, 0))
            pv = pbank[:, 0:260].rearrange("p (a b) -> p a b", a=4)
            for qs in range(4):
                c0 = g * 512 + qs * 128
                S.op("pe", lambda e, qs=qs, c0=c0: e.transpose(
                    out=pv[:, qs, :], in_=osb[0:65, c0:c0 + 128], identity=identf[0:65, 0:65]),
                    rd=[bosb, bconst], wr=[bpbank], inc=(qs == 3))
            sm, bsm = smring.next()
            S.op("dve", lambda e: e.reciprocal(out=sm[:, 0:4], in_=pv[:, :, 64]), rd=[bpbank], wr=[bsm])
            tv = tgt[:, g * 256:(g + 1) * 256].rearrange("p (a b) -> p a b", a=4)
            S.op("dve", lambda e: e.tensor_tensor(out=tv, in0=pv[:, :, 0:64], in1=bc(sm[:, 0:4], 2, [128, 4, 64]),
                                                  op=ALU.mult), rd=[bpbank, bsm], wr=[btgt])
            return tv

        def post_plain(cp, cols):
            cc, bcc = ccs[cp], bccs[cp]

            def part(j):
                tv = post_T_g(a1, ba1, j)
                cv = cc[:, :, cols[j]:cols[j] + 64]
                S.op("dve", lambda e: e.tensor_tensor(out=cv, in0=tv, in1=cv, op=ALU.mult), rd=[ba1, bcc], wr=[bcc])
            return [lambda: part(0), lambda: part(1)]

        def post_diffA():
            return [lambda: post_T_g(hA, bhA, 0), lambda: post_T_g(hA, bhA, 1)]

        def post_diffB(cp, t):
            cc, bcc = ccs[cp], bccs[cp]
            st = {}

            def p1():
                post_T_g(a1, ba1, 1)
                S.op("dve", lambda e: e.scalar_tensor_tensor(out=a2[:], in0=a1[:], scalar=nlam[:, 0:1], in1=hA[:],
                                                             op0=ALU.mult, op1=ALU.add),
                     rd=[ba1, bhA, bnlam], wr=[ba2])
                S.op("dve", lambda e: e.tensor_tensor(out=a1[:], in0=a2[:], in1=a2[:], op=ALU.mult), rd=[ba2], wr=[ba1])
                sm, bsm = smring.next()
                S.op("dve", lambda e: e.reduce_sum(out=sm[:, 0:8], in_=a1[:].rearrange("p (a d) -> p a d", a=8),
                                                   axis=AX.X), rd=[ba1], wr=[bsm])
                st["sm"] = (sm, bsm)

            def p2():
                pass

            def p3():
                sm, bsm = st["sm"]
                rs, brs = rstd_of(sm[:, 0:8], 64, bsm, 8)
                d3 = a2[:].rearrange("p (a d) -> p a d", a=8)
                S.op("dve", lambda e: e.tensor_tensor(out=d3, in0=d3, in1=bc(rs, 2, [128, 8, 64]), op=ALU.mult),
                     rd=[ba2, brs], wr=[ba2])
                S.op("dve", lambda e: e.tensor_tensor(out=d3, in0=d3, in1=bc(sg2[:, 0:64], 1, [128, 8, 64]),
                                                      op=ALU.mult), rd=[ba2, bsg2], wr=[ba2])
                cv = cc[:, :, t * 128:(t + 1) * 128].rearrange("p q (g d) -> p g q d", g=2)
                dv = a2[:].rearrange("p (g q d) -> p g q d", g=2, q=4)
                S.op("dve", lambda e: e.tensor_tensor(out=cv, in0=dv, in1=cv, op=ALU.mult), rd=[ba2, bcc], wr=[bcc])
            return [lambda: post_T_g(a1, ba1, 0), p1, p2, p3]

        def attention(cp, bg, rate=1):
            QT, bQT = QTs[cp], bQTs[cp]

            nstep = [0]

            def bgstep():
                if bg is not None:
                    nstep[0] += 1
                    for _ in range(rate):
                        next(bg, None)
                    if nstep[0] % 2 == 0:
                        next(bg, None)
            passes = []

            def diff_pass(t, which):
                if True:
                    es = [dict(kt=lambda kt, rr_=rr_, t=t: KT[rr_:rr_ + 64, t, kt * 128:(kt + 1) * 128],
                               q=QT[rr_:rr_ + 64, 4 * which + t, :],
                               v=lambda kt, h=2 * t + rr_ // 64: Vflat[:, kt * 520 + h * 65:kt * 520 + h * 65 + 128],
                               row=rr_, rd=[bKT, bQT], rdv=[bV]) for rr_ in (0, 64)]
                    passes.append((32, 32 ** -0.5, es,
                                   post_diffA if which == 0 else (lambda t=t: post_diffB(cp, t))))

            def gqa_pass(j):
                es = [dict(kt=lambda kt, rr_=rr_: KT[rr_:rr_ + 64, 3, kt * 128:(kt + 1) * 128],
                           q=QT[rr_:rr_ + 64, 7 + j, :],
                           v=lambda kt, rr_=rr_: Vflat[:, kt * 520 + (6 + rr_ // 64) * 65:kt * 520 + (6 + rr_ // 64) * 65 + 128],
                           row=rr_, rd=[bKT, bQT], rdv=[bV]) for rr_ in (0, 64)]
                passes.append((32, 0.125, es, lambda j=j: post_plain(cp, (384 + j * 64, 384 + (j + 3) * 64))))

            def mem_pass(mb):
                es = [dict(kt=lambda kt, rr_=rr_, mb=mb: KmT[rr_:rr_ + 64, mb, kt * 128:(kt + 1) * 128],
                           q=QT[rr_:rr_ + 64, 3 + 7 * mb, :],
                           v=lambda kt, rr_=rr_, mb=mb: Vmflat[:, kt * 260 + (2 * mb + rr_ // 64) * 65:
                                                               kt * 260 + (2 * mb + rr_ // 64) * 65 + 128],
                           row=rr_, rd=[bKmT, bQT], rdv=[bVm]) for rr_ in (0, 64)]
                passes.append((2, 0.125, es, lambda mb=mb: post_plain(cp, (768 + mb * 128, 768 + mb * 128 + 64))))

            diff_pass(0, 0)
            mem_pass(0)
            diff_pass(0, 1)
            diff_pass(1, 0)
            mem_pass(1)
            diff_pass(1, 1)
            diff_pass(2, 0)
            diff_pass(2, 1)
            for j in range(3):
                gqa_pass(j)

            import collections
            hooks = collections.deque()
            for k, (nkt, scale, es, post) in enumerate(passes):
                nxt_es = passes[k + 1][2] if k + 1 < len(passes) else None
                if nkt < 8:
                    while hooks:
                        hooks.popleft()()
                attn_pass(nkt, scale, es, bgstep, hooks, k == 0, nxt_es)
                assert not hooks or nkt < 8
                hooks.extend(post())
            while hooks:
                hooks.popleft()()

        def g_silu(ts, pg, bpg, cp, ti, segs):
            (t1, t2, _), (bt1, bt2, _) = tts[ts], btts[ts]
            cc, bcc = ccs[cp], bccs[cp]
            S.op("act", lambda e: e.activation(out=t1[:], in_=pg, func=AF.Exp, scale=-1.0), rd=[bpg], wr=[bt1])
            S.op("dve", lambda e: e.tensor_copy(out=t2[:], in_=pg), rd=[bpg], wr=[bt2])
            yield
            yield
            S.op("dve", lambda e: e.tensor_scalar(out=t1[:], in0=t1[:], scalar1=1.0, scalar2=None, op0=ALU.add),
                 rd=[bt1], wr=[bt1])
            S.op("dve", lambda e: e.reciprocal(out=t1[:], in_=t1[:]), rd=[bt1], wr=[bt1])
            for c0, n, d0 in segs:
                S.op("dve", lambda e, c0=c0, n=n, d0=d0: e.tensor_tensor(
                    out=cc[:, ti, d0:d0 + n], in0=t2[:, c0:c0 + n], in1=t1[:, c0:c0 + n], op=ALU.mult),
                    rd=[bt1, bt2], wr=[bcc])
            yield
            yield

        def g_kv_tile(l, par, i, ld, wk, bwk, wv, bwv, ts, BS, xT_ap, bxT, hbr=None, sgr=None, alt="dve"):
            tab, btab = ld[2]
            yield from g_front(ld, gpre, bgpre, xT_ap, bxT, BS["T1"], ts, hbr, alt)
            pk, bpk = proj(xT_ap, bxT, wk, bwk, BS["PA"])
            yield
            yield
            yield
            sg_t, bstg = (sgr or stgring).next()
            rope(ts, pk[:, 0:384], [bpk], 12, 1, tab[:, 0:32], tab[:, 32:64], btab, sg_t[:, 0:384], bstg)
            yield
            yield from g_gqa_norm_rope(ts, pk[:, 384:512], [bpk], 2, sg[:, 128:192], tab, btab,
                                       lambda tn, rs, sg_t=sg_t: (sg_t[:, 384:512].rearrange("p (h d) -> p h d", h=2),
                                                                  tn.rearrange("p (h d) -> p h d", h=2),
                                                                  bc(rs, 2, [128, 2, 64])), bstg)
            yield
            yield
            yield
            yield
            pv, bpv = transposes_bf16(sg_t, bstg, 4, BS["T2"])
            yield
            yield
            yield
            pc, tq = i // 4, i % 4
            kt_t, bkt = ktring.next()
            if alt == "act":
                S.op("act", lambda e: e.copy(out=kt_t[:], in_=pv[:, 0:4, :]), rd=[bpv], wr=[bkt])
            else:
                S.op("dve", lambda e: e.tensor_copy(out=kt_t[:], in_=pv[:, 0:4, :]), rd=[bpv], wr=[bkt])
            S.dma("pool", ksnd[par][pc].rearrange("p (b k) -> p b k", b=4)[:, :, tq * 128:(tq + 1) * 128], kt_t[:],
                  rd=[bkt], wr=[bkvsK[par][i]])
            yield
            yield
            pvv, bpvv = proj(xT_ap, bxT, wv, bwv, BS["PB"])
            yield
            yield
            yield
            v_t, bvt = vring.next()
            if alt == "act":
                S.op("act", lambda e: e.copy(out=v_t[:, :, 0:64], in_=pvv.rearrange("p (h d) -> p h d", h=8)),
                     rd=[bpvv], wr=[bvt])
            else:
                S.op("dve", lambda e: e.tensor_copy(out=v_t[:, :, 0:64], in_=pvv.rearrange("p (h d) -> p h d", h=8)),
                     rd=[bpvv], wr=[bvt])
            S.dma("pool", vsnd[par][pc][:, tq * 520:(tq + 1) * 520], v_t[:].rearrange("p h d -> p (h d)"),
                  rd=[bvt], wr=[bkvsV[par][i]])
            if tq == 3:
                for kind, snd_, rcv_, bs_, br_ in (("k", ksnd, krcv, bkvsK, bkrcv), ("v", vsnd, vrcv, bkvsV, bvrcv)):
                    S.custom("pool", lambda e, snd_=snd_, rcv_=rcv_: e.collective_compute(
                        "AllGather", ALU.bypass, replica_groups=[[0, 1], [2, 3], [4, 5], [6, 7]],
                        ins=[snd_[par][pc]], outs=[rcv_[par][pc]]),
                        ccsems[l * 8 + pc * 2 + (kind == "v")], rd=bs_[par][pc * 4:pc * 4 + 4], wr=[br_[par][pc]])
            yield
            yield

        def kv_weights(l):
            return (wload([(w_in[l, :, 384:768], 0), (w_in[l, :, 1920:2048], 384)]),
                    wload([(w_in[l, :, 768:1152], 0), (w_in[l, :, 2048:2176], 384)]))

        def early_params(l):
            S.dma("sp", gpre[:], pre_n[l:l + 1, :].broadcast_to([128, D]), wr=[bgpre])
            S.dma("sp", sg[:], sgn[l:l + 1, :].broadcast_to([128, 192]), wr=[bsg])

        def g_early(l):
            early_params(l)
            yield

        def g_kv_bg(l, tiles, xp):
            (wk, bwk), (wv, bwv) = kv_weights(l)
            yield
            lds = {tiles[0]: xload(out[tiles[0] * 128:(tiles[0] + 1) * 128, :], [bxrow[tiles[0]]], tabo[tiles[0]])}
            for k, i in enumerate(tiles):
                if k + 1 < len(tiles):
                    j = tiles[k + 1]
                    lds[j] = xload(out[j * 128:(j + 1) * 128, :], [bxrow[j]], tabo[j])
                ld = lds[i]
                yield from g_kv_tile(l, l % 2, i, ld, wk, bwk, wv, bwv, 0, BS_BG[0], xTcs[xp][:, i % TPC],
                                     bxTs[xp][i % TPC])

        def kv_seq4(l, tiles):
            xsrc = xown if l == 0 else out
            (wk, bwk), (wv, bwv) = kv_weights(l)
            OFF = 9

            def start(k):
                i = tiles[k]
                s_ = k % 4
                rdx = [bxrow[i]] if l > 0 else []
                ld = xload(xsrc[i * 128:(i + 1) * 128, :], rdx, tabo[i], xs_kv, tab_kv)
                return g_kv_tile(l, l % 2, i, ld, wk, bwk, wv, bwv, s_, BS_KV4[s_],
                                 xTcs[s_ % 2][:, s_ // 2 + 2 * ((k // 4) % 2)],
                                 bxTs[s_ % 2][s_ // 2 + 2 * ((k // 4) % 2)], hb_kv, stg_kv, "act")
            active = [None] * 4
            nxt, step = 0, 0
            while nxt < len(tiles) or any(g is not None for g in active):
                s_ = nxt % 4
                if nxt < len(tiles) and active[s_] is None and step >= nxt * OFF:
                    active[s_] = start(nxt)
                    nxt += 1
                for k_ in range(4):
                    if active[k_] is not None:
                        try:
                            next(active[k_])
                        except StopIteration:
                            active[k_] = None
                step += 1

        def kv_seq(l, tiles):
            xsrc = xown if l == 0 else out
            (wk, bwk), (wv, bwv) = kv_weights(l)

            def xl(i):
                rdx = [bxrow[i]] if l > 0 else []
                return xload(xsrc[i * 128:(i + 1) * 128, :], rdx, tabo[i])
            lds = {tiles[0]: xl(tiles[0]), tiles[1]: xl(tiles[1])}
            for k0 in range(0, len(tiles), 2):
                i0, i1 = tiles[k0], tiles[k0 + 1]
                if k0 > 0:
                    lds[i1] = xl(i1)
                if k0 + 2 < len(tiles):
                    lds[tiles[k0 + 2]] = xl(tiles[k0 + 2])
                gens = [g_kv_tile(l, l % 2, i, lds[i], wk, bwk, wv, bwv, i % 2, BS_SEQ[i % 2],
                                  xTcs[i % 2][:, (i // 2) % TPC], bxTs[i % 2][(i // 2) % TPC]) for i in (i0, i1)]
                drain(rr(*gens))

        def g_q_tile(gidx, cp, ti, tab_src, w, bw, ts, BS):
            tabld = None
            if tab_src is not None:
                tabld = tabring.next()
                S.dma("sp", tabld[0], tab_src, wr=[tabld[1]])
            xT_ap, bxT = xTcs[cp][:, ti], bxTs[cp][ti]
            QT, bQT = QTs[cp], bQTs[cp]
            pq, bpq = proj(xT_ap, bxT, w, bw, BS["PA"])
            yield
            yield
            yield
            if gidx == 0:
                tab, btab = tabld
                (t1, t2, _), (bt1, bt2, _) = tts[ts], btts[ts]
                sA, sB = stgQ[ts]
                bsq_ = bstgQ[ts]
                rope(ts, pq[:, 0:384], [bpq], 12, 1, tab[:, 0:32], tab[:, 32:64], btab, None, None)
                S.op("dve", lambda e: e.tensor_copy(out=sA[:, 384:512], in_=pq[:, 384:512]), rd=[bpq], wr=[bsq_])
                v3 = lambda ap: ap.rearrange("p (h w d) -> p h w d", h=6, w=2)
                for w_, sX in ((0, sA), (1, sB)):
                    S.op("dve", lambda e, w_=w_, sX=sX: e.tensor_tensor(
                        out=v3(sX[:, 0:384])[:, :, w_, :], in0=v3(t1[:, 0:384])[:, :, w_, :],
                        in1=v3(t2[:, 0:384])[:, :, w_, :], op=ALU.add), rd=[bt1, bt2], wr=[bsq_])
                yield
                yield
                yield
                yield
                pbank, bpbank = bk(BS["T2"])
                pv = bf16view(pbank)
                for k_, (sX, j) in enumerate([(sA, 0), (sA, 1), (sA, 2), (sA, 3), (sB, 0), (sB, 1), (sB, 2)]):
                    S.op("pe", lambda e, k_=k_, sX=sX, j=j: e.transpose(
                        out=pv[:, k_, :], in_=sX[:, j * 128:(j + 1) * 128], identity=ident[:]),
                        rd=[bsq_, bconst], wr=[bpbank], inc=(k_ == 6))
                yield
                yield
                S.op("dve", lambda e: e.tensor_copy(out=QT[:, 0:7, ti * 128:(ti + 1) * 128], in_=pv[:, 0:7, :]),
                     rd=[bpbank], wr=[bQT])
                yield
                yield
            elif gidx == 1:
                tab, btab = tabld
                sg_t, bstg = stgring.next()
                S.op("dve", lambda e: e.tensor_copy(out=sg_t[:, 384:512], in_=pq[:, 384:512]), rd=[bpq], wr=[bstg])
                yield from g_gqa_norm_rope(ts, pq[:, 0:384], [bpq], 6, sg[:, 64:128], tab, btab,
                                           lambda tn, rs, sg_t=sg_t: (
                                               sg_t[:, 0:384].rearrange("p (j h d) -> p h j d", j=3, h=2),
                                               tn.rearrange("p (h j d) -> p h j d", h=2, j=3),
                                               bc(rs.rearrange("p (h j) -> p h j", h=2), 3, [128, 2, 3, 64])), bstg)
                yield
                yield
                yield
                yield
                pv, bpv = transposes_bf16(sg_t, bstg, 4, BS["T2"])
                yield
                yield
                S.op("dve", lambda e: e.tensor_copy(out=QT[:, 7:11, ti * 128:(ti + 1) * 128], in_=pv[:, 0:4, :]),
                     rd=[bpv], wr=[bQT])
                yield
                yield
            else:
                segs = [(0, 384, 0), (384, 128, 768)] if gidx == 2 else [(0, 384, 384), (384, 128, 896)]
                yield from g_silu(ts, pq, bpq, cp, ti, segs)
            yield

        def g_phaseQ(l, c, BSs):
            cp = c % 2
            xsrc = xown if l == 0 else out

            def xl(i, with_tab):
                rdx = [bxrow[i]] if l > 0 else []
                return xload(xsrc[i * 128:(i + 1) * 128, :], rdx, tabo[i] if with_tab else None)
            groups = [
                [(w_in[l, :, 0:384], 0), (w_in[l, :, 2560:2688], 384)],
                [(w_in[l, :, 1536:1920], 0), (w_in[l, :, 2688:2816], 384)],
                [(w_in[l, :, 1152:1536], 0), (w_in[l, :, 2816:2944], 384)],
                [(w_in[l, :, 2176:2560], 0), (w_in[l, :, 2944:3072], 384)],
            ]
            nxtw = wload(groups[0])
            lds = [xl(c * TPC, False), xl(c * TPC + 1, False)]
            for ti in range(TPC):
                ld = lds[ti]
                if ti + 2 < TPC:
                    lds.append(xl(c * TPC + ti + 2, False))
                yield from g_front(ld, gpre, bgpre, xTcs[cp][:, ti], bxTs[cp][ti], BSs[ti % 2]["T1"], ti % 2)
            for gidx, parts in enumerate(groups):
                w, bw = nxtw
                if gidx + 1 < len(groups):
                    nxtw = wload(groups[gidx + 1])
                gens = []
                for ti in range(TPC):
                    gens.append(g_q_tile(gidx, cp, ti, tabo[c * TPC + ti] if gidx in (0, 1) else None,
                                         w, bw, ti % 2, BSs[ti % 2]))
                    if BSs[0] is BSs[1]:
                        yield from gens[ti]
                    elif ti % 2 == 1:
                        yield from rr(gens[ti - 1], gens[ti])

        def g_c_tile(l, c, ti, wo, BS, ts, yb=None):
            ysb, bysb = yb if yb is not None else (ysb_main, bysb_main)
            cp = c % 2
            cc, bcc = ccs[cp], bccs[cp]
            xsrc = xown if l == 0 else out
            i = c * TPC + ti
            rdx = [bxrow[i]] if l > 0 else []
            xs, bxs, _ = xload(xsrc[i * 128:(i + 1) * 128, :], rdx)
            tb, btb = bk(BS["T1"])
            pv = bf16view(tb)
            for kc in range(8):
                S.op("pe", lambda e, kc=kc: e.transpose(
                    out=pv[:, kc, :], in_=cc[:, ti, kc * 128:(kc + 1) * 128], identity=ident[:]),
                    rd=[bcc, bconst], wr=[btb], inc=(kc == 7))
            yield
            yield
            xT_ap, bxT = xTcs[cp][:, ti], bxTs[cp][ti]
            S.op("dve", lambda e: e.tensor_copy(out=xT_ap, in_=pv), rd=[btb], wr=[bxT])
            yield
            yield
            for g in range(2):
                py, bpy = proj(xT_ap, bxT, wo[g][0], wo[g][1], BS["PA"] if g == 0 else BS["T2"])
                yield
                yield
                S.op("dve", lambda e, g=g, py=py: e.tensor_copy(out=ysb[:, g * 512:(g + 1) * 512], in_=py),
                     rd=[bpy], wr=[bysb])
                yield
                yield
            sqj, bsq = TT[ts], [btts[ts][0], btts[ts][1]]
            S.op("dve", lambda e: e.tensor_tensor(out=sqj[:], in0=ysb[:], in1=ysb[:], op=ALU.mult),
                 rd=[bysb], wr=bsq)
            sm, bsm = smring.next()
            S.op("dve", lambda e: e.reduce_sum(out=sm[:, 0:1], in_=sqj[:], axis=AX.X), rd=bsq, wr=[bsm])
            yield
            yield
            yield
            yield
            rs, brs = rstd_of(sm[:, 0:1], D, bsm, 1)
            yield
            yield
            S.op("dve", lambda e: e.scalar_tensor_tensor(out=ysb[:], in0=ysb[:], scalar=rs, in1=gpost[:],
                                                         op0=ALU.mult, op1=ALU.mult),
                 rd=[bysb, brs, bgpost], wr=[bysb])
            S.op("dve", lambda e: e.tensor_tensor(out=xs[:], in0=xs[:], in1=ysb[:], op=ALU.add),
                 rd=[bxs, bysb], wr=[bxs])
            S.dma("pool", out[i * 128:(i + 1) * 128, :], xs[:], rd=[bxs], wr=[bxrow[i]])
            yield

        def load_wo(l):
            return [wload([(w_out[l, :, 0:512], 0)]), wload([(w_out[l, :, 512:1024], 0)])]

        def g_load_wo(l, box):
            box.append(load_wo(l))
            yield

        def g_phaseC(l, c, BSs, wo):
            if BSs[0] is BSs[1]:
                for ti in range(TPC):
                    yield from g_c_tile(l, c, ti, wo, BSs[ti % 2], ti % 2)
            else:
                ybs = [(ysb_main, bysb_main), (osb, bosb)]
                for t0 in range(0, TPC, 2):
                    yield from rr(*[g_c_tile(l, c, ti, wo, BSs[ti % 2], ti % 2, ybs[ti % 2]) for ti in (t0, t0 + 1)])

        for l in range(nl):
            xsrc = xown if l == 0 else out
            par = l % 2
            if l == 0:
                early_params(0)
            kv_seq4(l, list(range(NBG_KV if l > 0 else 0, NT_OWN)))

            S.dma("sp", gmem[:], mem_n[l:l + 1, :].broadcast_to([128, D]), wr=[bgmem])
            S.dma("sp", gpost[:], post_n[l:l + 1, :].broadcast_to([128, D]), wr=[bgpost])
            S.dma("sp", dl[:], dlam[l:l + 1, :].broadcast_to([128, 128]), wr=[bdl])
            S.dma("sp", lin_t[:], lini[l:l + 1, :].broadcast_to([128, 2]), wr=[blin])
            S.op("dve", lambda e: e.tensor_tensor(out=a1[:, 0:32], in0=dl[:, 0:32], in1=dl[:, 32:64], op=ALU.mult),
                 rd=[bdl], wr=[ba1])
            S.op("dve", lambda e: e.tensor_tensor(out=a1[:, 32:64], in0=dl[:, 64:96], in1=dl[:, 96:128], op=ALU.mult),
                 rd=[bdl], wr=[ba1])
            sm, bsm = smring.next()
            S.op("dve", lambda e, sm=sm: e.reduce_sum(out=sm[:, 0:2], in_=a1[:, 0:64].rearrange("p (a b) -> p a b", a=2),
                                                      axis=AX.X), rd=[ba1], wr=[bsm])
            S.op("act", lambda e, sm=sm: e.activation(out=sm[:, 2:4], in_=sm[:, 0:2], func=AF.Exp), rd=[bsm], wr=[bsm])
            S.op("dve", lambda e, sm=sm: e.tensor_tensor(out=nlam[:], in0=sm[:, 3:4], in1=sm[:, 2:3], op=ALU.subtract),
                 rd=[bsm], wr=[bnlam])
            S.op("dve", lambda e: e.tensor_scalar(out=nlam[:], in0=nlam[:], scalar1=lin_t[:, 0:1], scalar2=None,
                                                  op0=ALU.add), rd=[bnlam, blin], wr=[bnlam])
            S.op("dve", lambda e: e.tensor_scalar(out=sg2[:], in0=sg[:, 0:64], scalar1=lin_t[:, 1:2], scalar2=None,
                                                  op0=ALU.mult), rd=[bsg, blin], wr=[bsg2])

            for pc in range(3):
                for r in range(2):
                    S.dma("sp", KT[:, :, r * 2048 + pc * 512:r * 2048 + (pc + 1) * 512],
                          krcv[par][pc][r * 128:(r + 1) * 128, :].rearrange("p (b k) -> p b k", b=4),
                          rd=[bkrcv[par][pc]], wr=[bKT])
                    S.dma("sp", V[:, r * 16 + pc * 4:r * 16 + pc * 4 + 4].rearrange("p t h d -> p t (h d)"),
                          vrcv[par][pc][r * 128:(r + 1) * 128, :].rearrange("p (t f) -> p t f", t=4),
                          rd=[bvrcv[par][pc]], wr=[bV])

            wm, bwm = wload([(w_mkv[l], 0)])
            for mt in range(2):
                ld = xload(mem[mt * 128:(mt + 1) * 128, :], [])
                xT_ap, bxT = xTcs[1][:, mt], bxTs[1][mt]
                drain(g_front(ld, gmem, bgmem, xT_ap, bxT, (3, 0), 0))
                pm, bpm = proj(xT_ap, bxT, wm, bwm, (2, mt))
                sg_t, bstg = stgring.next()
                S.op("dve", lambda e, sg_t=sg_t, pm=pm: e.tensor_copy(out=sg_t[:, 0:256], in_=pm[:, 0:256]),
                     rd=[bpm], wr=[bstg])
                pv, bpv = transposes_bf16(sg_t, bstg, 2, (3, 1))
                S.op("dve", lambda e, pv=pv, mt=mt: e.tensor_copy(out=KmT[:, :, mt * 128:(mt + 1) * 128],
                                                                  in_=pv[:, 0:2, :]), rd=[bpv], wr=[bKmT])
                S.op("dve", lambda e, pm=pm, mt=mt: e.tensor_copy(
                    out=Vm[:, mt, :, 0:64], in_=pm[:, 256:512].rearrange("p (h d) -> p h d", h=4)),
                    rd=[bpm], wr=[bVm])

            for pc in range(3, 4):
                for r in range(2):
                    S.dma("sp", KT[:, :, r * 2048 + pc * 512:r * 2048 + (pc + 1) * 512],
                          krcv[par][pc][r * 128:(r + 1) * 128, :].rearrange("p (b k) -> p b k", b=4),
                          rd=[bkrcv[par][pc]], wr=[bKT])
                    S.dma("sp", V[:, r * 16 + pc * 4:r * 16 + pc * 4 + 4].rearrange("p t h d -> p t (h d)"),
                          vrcv[par][pc][r * 128:(r + 1) * 128, :].rearrange("p (t f) -> p t f", t=4),
                          rd=[bvrcv[par][pc]], wr=[bV])

            import os
            if l == 0:
                drain(g_phaseQ(l, 0, BS_SEQ))
            if "seq" in os.environ.get("KDBG", ""):
                for c in range(NCH):
                    if c > 0:
                        drain(g_phaseQ(l, c, BS_SEQ))
                    attention(c % 2, None)
                    drain(g_phaseC(l, c, BS_SEQ, load_wo(l)))
                continue
            DBG = os.environ.get("KDBG", "")
            wos = []
            for c in range(NCH):
                nxt = (c == NCH - 1 and l + 1 < nl)
                bg = chain(g_phaseC(l, c - 1, BS_BG, wos[c - 1]) if c > 0 else None,
                           g_phaseQ(l, c + 1, BS_BG) if c + 1 < NCH else None,
                           g_early(l + 1) if nxt else None,
                           g_phaseQ(l + 1, 0, BS_BG) if nxt else None,
                           g_kv_bg(l + 1, list(range(NBG_KV)), (c - 1) % 2) if (nxt and NBG_KV) else None,
                           g_load_wo(l, wos))
                attention(c % 2, bg, 1)
                drain(bg)
            drain(g_phaseC(l, NCH - 1, BS_SEQ, wos[NCH - 1]))

        S.wait_all("pool", bxrow)
        S.emit(block)
    return nc


def _rope_tab(pos, dim):
    inv = (10000.0 ** (-np.arange(0, dim, 2, dtype=np.float32) / np.float32(dim))).astype(np.float32)
    ang = pos.astype(np.float32)[:, None] * inv[None, :]
    return np.cos(ang).astype(np.float32), np.sin(ang).astype(np.float32)


def _tables():
    t = np.arange(4096)
    lc, ls = _rope_tab(t, 32)
    rc, rs = _rope_tab(t // 64, 32)
    cc_, cs = _rope_tab(t % 64, 32)
    tab = np.concatenate([lc, lc, -ls, ls, rc, rc, cc_, cc_, -rs, rs, -cs, cs], axis=1).astype(np.float32)
    return tab.reshape(32, 128, 192)


_CACHE = {}


def _get_nc(nl):
    if nl not in _CACHE:
        _CACHE[nl] = build(nl)
    return _CACHE[nl]


FUSED = True


def _in_maps(x_cur, mem, layers, p, tab):
    nl = len(layers)
    f = lambda a: np.ascontiguousarray(a, dtype=np.float32)
    sl = lambda a: f(a[layers])
    lin = np.array([[-(0.8 - 0.6 * math.exp(-0.3 * l)), 1.0 - (0.8 - 0.6 * math.exp(-0.3 * l))] for l in layers],
                   dtype=np.float32)
    shared = {
        "w_in": sl(p["w_in"]), "w_mkv": sl(p["w_mem_kv"]), "w_out": sl(p["w_out"]),
        "pre_n": sl(p["pre_norm"]), "mem_n": sl(p["mem_norm"]), "post_n": sl(p["post_norm"]),
        "dlam": sl(p["diff_lambda"]).reshape(nl, 128),
        "sgn": f(np.concatenate([p["diff_subln"][layers], p["gqa_q_norm"][layers], p["gqa_k_norm"][layers]], axis=1)),
        "lini": lin,
        "ident": np.eye(128).astype(ml_dtypes.bfloat16),
        "identf": np.eye(128, dtype=np.float32),
    }
    maps = []
    for c in range(8):
        b, r = c // 2, c % 2
        m = dict(shared)
        m["xown"] = f(x_cur[b, r * S_OWN:(r + 1) * S_OWN])
        m["mem"] = f(mem[b])
        m["tabo"] = f(tab[r * 16:(r + 1) * 16])
        maps.append(m)
    return maps


def _run(x_cur, mem, layers, p, tab):
    nc = _get_nc(len(layers))
    res = run_bass_kernel_spmd(nc, _in_maps(x_cur, mem, layers, p, tab), core_ids=list(range(8)))
    y = np.empty_like(x_cur)
    for c in range(8):
        b, r = c // 2, c % 2
        y[b, r * S_OWN:(r + 1) * S_OWN] = res.results[c]["out"]
    return y


def kernel(x, mem, pre_norm, w_in, diff_lambda, diff_subln, gqa_q_norm, gqa_k_norm,
           mem_norm, w_mem_kv, w_out, post_norm, _layers=None):
    p = dict(pre_norm=np.asarray(pre_norm), w_in=np.asarray(w_in), diff_lambda=np.asarray(diff_lambda),
             diff_subln=np.asarray(diff_subln), gqa_q_norm=np.asarray(gqa_q_norm),
             gqa_k_norm=np.asarray(gqa_k_norm), mem_norm=np.asarray(mem_norm), w_mem_kv=np.asarray(w_mem_kv),
             w_out=np.asarray(w_out), post_norm=np.asarray(post_norm))
    x = np.asarray(x, dtype=np.float32)
    mem = np.asarray(mem, dtype=np.float32)
    tab = _tables()
    depth = p["w_in"].shape[0] if _layers is None else _layers
    if FUSED:
        return _run(x, mem, list(range(depth)), p, tab)
    for l in range(depth):
        x = _run(x, mem, [l], p, tab)
    return x
```
